# Optimizing a Trainium2 kernel written in Bass

```python
import math
import jax, jax.numpy as jnp
from jax import lax
import numpy as np

D_MODEL = 4096
BATCH = 1
SEQ = 8192
DEPTH = 4

N_A_LAYERS = DEPTH // 2
N_B_LAYERS = DEPTH - N_A_LAYERS
MIX_HEAD_DIM = 128
MIX_WIDTH = 3 * D_MODEL // 4
MIX_HEADS = MIX_WIDTH // MIX_HEAD_DIM
MEM_HEADS = 4
MEM_WIDTH = D_MODEL - MIX_WIDTH
MEM_HEAD_DIM = MEM_WIDTH // MEM_HEADS
N_MEM = 256
HGRN_CHUNK = 64
SB_BLOCK = 128
D_FF = 256 * ((8 * D_MODEL // 3 + 255) // 256)
CONV_WIDTH = 3
LN_EPS = 1e-5
RMS_EPS = 1e-6
LB_TINY = 1e-30
DEEPNORM_ALPHA = (2 * DEPTH) ** 0.25
DEEPNORM_BETA = (8 * DEPTH) ** -0.25
A_IN_WIDTH = 4 * MIX_WIDTH + MEM_WIDTH
B_IN_WIDTH = MIX_WIDTH + MEM_WIDTH

kernel_name = "yoco_hgrn2_stickbreaking_hybrid"


def layer_norm(x, g, b):
    xf = x.astype(jnp.float32)
    mu = xf.mean(-1, keepdims=True)
    var = jnp.square(xf - mu).mean(-1, keepdims=True)
    return ((xf - mu) * lax.rsqrt(var + LN_EPS) * g + b).astype(x.dtype)


def split_heads(t, head_dim):
    return t.reshape(t.shape[0], t.shape[1], -1, head_dim)


def hgrn2_chunkwise(q, k, v, log_f):
    B, S, H, DK = q.shape
    DV = v.shape[-1]
    nc = S // HGRN_CHUNK

    def to_chunks(t):
        return t.astype(jnp.float32).reshape(B, nc, HGRN_CHUNK, H, t.shape[-1]).transpose(1, 0, 3, 2, 4)

    qc, kc, vc, gc = to_chunks(q), to_chunks(k), to_chunks(v), to_chunks(log_f)
    causal = jnp.tril(jnp.ones((HGRN_CHUNK, HGRN_CHUNK), dtype=bool))[:, :, None]

    def step(state, inp):
        qb, kb, vb, gb = inp
        b = jnp.cumsum(gb, axis=-2)
        diff = b[..., :, None, :] - b[..., None, :, :]
        decay = jnp.where(causal, jnp.exp(jnp.where(causal, diff, 0.0)), 0.0)
        scores = jnp.einsum('bhtd,bhtsd,bhsd->bhts', qb, decay, kb)
        o = (jnp.einsum('bhts,bhsv->bhtv', scores, vb)
             + jnp.einsum('bhtd,bhdv->bhtv', qb * jnp.exp(b), state))
        b_last = b[..., -1:, :]
        new_state = (jnp.exp(b_last)[..., 0, :, None] * state
                     + jnp.einsum('bhsd,bhsv->bhdv', kb * jnp.exp(b_last - b), vb))
        return new_state, o

    state0 = jnp.zeros((B, H, DK, DV), jnp.float32)
    _, out = lax.scan(step, state0, (qc, kc, vc, gc))
    return out.transpose(1, 0, 3, 2, 4).reshape(B, S, H, DV).astype(v.dtype)


def stick_breaking_attention(q, k, v):
    B, S, H, D = q.shape
    nb = S // SB_BLOCK
    scale = D ** -0.5
    kh = k.transpose(0, 2, 1, 3)
    vh = v.transpose(0, 2, 1, 3)
    qb = q.reshape(B, nb, SB_BLOCK, H, D).transpose(1, 0, 3, 2, 4)
    key_pos = jnp.arange(S)

    def one_block(args):
        blk, qblk = args
        q_pos = blk * SB_BLOCK + jnp.arange(SB_BLOCK)
        mask = key_pos[None, :] < q_pos[:, None]
        z = jnp.einsum('bhqd,bhsd->bhqs', qblk, kh).astype(jnp.float32) * scale
        log_beta = jax.nn.log_sigmoid(z)
        log_rest = jnp.where(mask, jax.nn.log_sigmoid(-z), 0.0)
        after = lax.cumsum(log_rest, axis=3, reverse=True) - log_rest
        w = jnp.where(mask, jnp.exp(jnp.where(mask, log_beta + after, 0.0)), 0.0)
        return jnp.einsum('bhqs,bhsd->bhqd', w.astype(v.dtype), vh)

    out = lax.map(one_block, (jnp.arange(nb), qb))
    return out.transpose(1, 0, 3, 2, 4).reshape(B, S, H, D)


def memory_cross_attention(mq, mem, w_mem_kv):
    B, S, _ = mq.shape
    mk, mv = jnp.split(mem @ w_mem_kv, 2, axis=-1)
    qh, kh, vh = split_heads(mq, MEM_HEAD_DIM), split_heads(mk, MEM_HEAD_DIM), split_heads(mv, MEM_HEAD_DIM)
    s = jnp.einsum('bqhd,bmhd->bhqm', qh, kh).astype(jnp.float32) * (MEM_HEAD_DIM ** -0.5)
    p = jax.nn.softmax(s, axis=-1).astype(mq.dtype)
    return jnp.einsum('bhqm,bmhd->bqhd', p, vh).reshape(B, S, MEM_WIDTH)


def hgrn2_mixer(x, mem, w_in, lb, onorm_g, w_mem_kv, w_o):
    B, S, _ = x.shape
    q, fz, i, g, mq = jnp.split(x @ w_in, [MIX_WIDTH, 2 * MIX_WIDTH, 3 * MIX_WIDTH, 4 * MIX_WIDTH], axis=-1)
    q = jax.nn.silu(q)
    fz = fz.astype(jnp.float32)
    log_f = jnp.logaddexp(jnp.log(jnp.maximum(lb, LB_TINY)), jnp.log1p(-lb) + jax.nn.log_sigmoid(fz))
    k = (1.0 - lb) * jax.nn.sigmoid(-fz)
    o = hgrn2_chunkwise(split_heads(q, MIX_HEAD_DIM), split_heads(k, MIX_HEAD_DIM),
                        split_heads(i, MIX_HEAD_DIM), split_heads(log_f, MIX_HEAD_DIM))
    of = o.astype(jnp.float32)
    of = of * lax.rsqrt(jnp.mean(jnp.square(of), axis=-1, keepdims=True) + RMS_EPS)
    o = (of.reshape(B, S, MIX_WIDTH) * onorm_g).astype(x.dtype) * jax.nn.silu(g)
    m = memory_cross_attention(mq, mem, w_mem_kv)
    return jnp.concatenate([o, m], axis=-1) @ w_o


def stick_breaking_mixer(x, mem, k_shared, v_shared, w_in, w_mem_kv, w_o):
    q, mq = jnp.split(x @ w_in, [MIX_WIDTH], axis=-1)
    B, S, _ = x.shape
    o = stick_breaking_attention(split_heads(q, MIX_HEAD_DIM), k_shared, v_shared).reshape(B, S, MIX_WIDTH)
    m = memory_cross_attention(mq, mem, w_mem_kv)
    return jnp.concatenate([o, m], axis=-1) @ w_o


def conv_glu_ffn(x, w_in, conv_w, conv_b, w_out):
    S = x.shape[1]
    gate, up = jnp.split(x @ w_in, 2, axis=-1)
    gp = jnp.pad(gate, ((0, 0), (CONV_WIDTH - 1, 0), (0, 0)))
    conv = conv_b + conv_w[0] * gp[:, 0:S]
    for tap in range(1, CONV_WIDTH):
        conv = conv + conv_w[tap] * gp[:, tap:tap + S]
    return (jax.nn.silu(conv) * up) @ w_out


def setup_inputs(seed: int = 0) -> dict:
    key = jax.random.key(seed)
    ks = jax.random.split(key, 16)

    def nrm(k, shape, scale):
        return jax.random.normal(k, shape, jnp.float32) * scale

    return {
        "x": nrm(ks[0], (BATCH, SEQ, D_MODEL), 1.0),
        "mem": nrm(ks[1], (BATCH, N_MEM, D_MODEL), 1.0),
        "a_w_in": nrm(ks[2], (N_A_LAYERS, D_MODEL, A_IN_WIDTH), D_MODEL ** -0.5),
        "hgrn_lb_logits": nrm(ks[3], (N_A_LAYERS, MIX_WIDTH), 0.5),
        "a_onorm_g": 1.0 + nrm(ks[4], (N_A_LAYERS, MIX_WIDTH), 0.02),
        "b_w_in": nrm(ks[5], (N_B_LAYERS, D_MODEL, B_IN_WIDTH), D_MODEL ** -0.5),
        "w_kv_shared": nrm(ks[6], (D_MODEL, 2 * MIX_WIDTH), D_MODEL ** -0.5),
        "w_mem_kv": nrm(ks[7], (DEPTH, D_MODEL, 2 * MEM_WIDTH), D_MODEL ** -0.5),
        "w_o": nrm(ks[8], (DEPTH, MIX_WIDTH + MEM_WIDTH, D_MODEL), DEEPNORM_BETA * D_MODEL ** -0.5),
        "ffn_w_in": nrm(ks[9], (DEPTH, D_MODEL, 2 * D_FF), D_MODEL ** -0.5),
        "ffn_conv_w": nrm(ks[10], (DEPTH, CONV_WIDTH, D_FF), CONV_WIDTH ** -0.5),
        "ffn_conv_b": nrm(ks[11], (DEPTH, D_FF), 0.02),
        "ffn_w_out": nrm(ks[12], (DEPTH, D_FF, D_MODEL), DEEPNORM_BETA * D_FF ** -0.5),
        "ln_g": 1.0 + nrm(ks[13], (DEPTH, 2, D_MODEL), 0.02),
        "ln_b": nrm(ks[14], (DEPTH, 2, D_MODEL), 0.02),
    }


def reference(x, mem, a_w_in, hgrn_lb_logits, a_onorm_g, b_w_in, w_kv_shared, w_mem_kv,
              w_o, ffn_w_in, ffn_conv_w, ffn_conv_b, ffn_w_out, ln_g, ln_b):
    B, S, _ = x.shape
    sm = jax.nn.softmax(hgrn_lb_logits.astype(jnp.float32), axis=0)
    lower_bounds = jnp.cumsum(sm, axis=0) - sm[0]
    k_shared = v_shared = None
    for layer in range(DEPTH):
        if layer < N_A_LAYERS:
            mix = hgrn2_mixer(x, mem, a_w_in[layer], lower_bounds[layer], a_onorm_g[layer],
                              w_mem_kv[layer], w_o[layer])
        else:
            if layer == N_A_LAYERS:
                kv = x @ w_kv_shared
                k_shared = split_heads(kv[..., :MIX_WIDTH], MIX_HEAD_DIM)
                v_shared = split_heads(kv[..., MIX_WIDTH:], MIX_HEAD_DIM)
            mix = stick_breaking_mixer(x, mem, k_shared, v_shared, b_w_in[layer - N_A_LAYERS],
                                       w_mem_kv[layer], w_o[layer])
        x = layer_norm(DEEPNORM_ALPHA * x + mix, ln_g[layer, 0], ln_b[layer, 0])
        ffn = conv_glu_ffn(x, ffn_w_in[layer], ffn_conv_w[layer], ffn_conv_b[layer], ffn_w_out[layer])
        x = layer_norm(DEEPNORM_ALPHA * x + ffn, ln_g[layer, 1], ln_b[layer, 1])
    return x
```

```python
import numpy as np
import ml_dtypes
from contextlib import ExitStack
import concourse.bass as bass
import concourse.mybir as mybir
from concourse.bass_utils import run_bass_kernel_spmd

AF = mybir.ActivationFunctionType
OP = mybir.AluOpType
AX = mybir.AxisListType
F32, BF16, I32 = mybir.dt.float32, mybir.dt.bfloat16, mybir.dt.int32


def make_cfg(D=4096, S=8192, DEPTH=4, NMEM=256, DFF=None, TBM=256, TF=512):
    c = dict(D=D, S=S, DEPTH=DEPTH, NMEM=NMEM, TBM=TBM, TF=TF)
    c["NA"] = DEPTH // 2
    c["MIXW"] = 3 * D // 4
    c["H"] = c["MIXW"] // 128
    c["MEMW"] = D - c["MIXW"]
    c["MHD"] = c["MEMW"] // 4
    c["DFF"] = DFF if DFF is not None else 256 * ((8 * D // 3 + 255) // 256)
    c["ALPHA"] = (2 * DEPTH) ** 0.25
    return c


class Buf:
    def __init__(self, t, dram=False):
        self.t = t
        self.w = {}
        self.r = {}
        self.dram = dram
        self.dsem = None
        self.dcnt = 0


class KB:
    def __init__(self, nc, st):
        self.nc, self.st = nc, st
        self.eng = {"pe": nc.tensor, "act": nc.scalar, "dve": nc.vector, "pool": nc.gpsimd, "sp": nc.sync}
        self.sem = {n: st.enter_context(nc.semaphore("s_" + n)) for n in self.eng}
        self.tick = {n: 0 for n in self.eng}
        self.seen = {n: {} for n in self.eng}
        self.semobj = {}
        self.dmax = {}
        self.nbuf = 0

    def sb(self, shape, dt, name=None):
        self.nbuf += 1
        return Buf(self.st.enter_context(self.nc.sbuf_tensor(f"{name or 'b'}{self.nbuf}", list(shape), dt)))

    def ps(self, shape, dt, name=None):
        self.nbuf += 1
        return Buf(self.st.enter_context(self.nc.psum_tensor(f"{name or 'p'}{self.nbuf}", list(shape), dt)))

    def _need(self, e, evs, out):
        for sid, val in evs.items():
            if e == "pe" and sid == "pe":
                continue
            if self.seen[e].get(sid, 0) < val and out.get(sid, 0) < val:
                out[sid] = val

    def _semh(self, sid):
        return self.semobj[sid] if sid in self.semobj else self.sem[sid]

    def _wait(self, e, evs):
        need = {}
        self._need(e, evs, need)
        for sid, val in need.items():
            self.eng[e].wait_ge(self._semh(sid), val)
            self.seen[e][sid] = val

    def _pre(self, e, R, W):
        need = {}
        for b in R:
            self._need(e, b.w, need)
        for b in W:
            self._need(e, b.w, need)
            self._need(e, b.r, need)
        items = list(need.items())
        for sid, val in items[:-1]:
            self.eng[e].wait_ge(self._semh(sid), val)
            self.seen[e][sid] = val
        if items:
            sid, val = items[-1]
            self.seen[e][sid] = val
            return (self._semh(sid), val)
        return None

    def _post(self, sid, val, R, W):
        for b in R:
            b.r[sid] = val
        for b in W:
            if b.dram:
                b.w[sid] = val
            else:
                b.w = {sid: val}
                b.r = {}

    def op(self, e, meth, R, W, *a, **kw):
        fw = self._pre(e, R, W)
        ins = getattr(self.eng[e], meth)(*a, **kw)
        if fw is not None:
            ins._wait_ge(*fw)
        self.tick[e] += 1
        ins.then_inc(self.sem[e], 1)
        self._post(e, self.tick[e], R, W)
        return ins

    def dma(self, q, out_b, out_ap, in_b, in_ap, **kw):
        sbb = in_b if out_b.dram else out_b
        if sbb.dsem is None:
            self.nbuf += 1
            sbb.dsem = f"d{self.nbuf}"
            self.semobj[sbb.dsem] = self.st.enter_context(self.nc.semaphore(sbb.dsem))
        fw = self._pre(q, [in_b], [out_b])
        ins = self.eng[q].dma_start(out=out_ap, in_=in_ap, **kw)
        if fw is not None:
            ins._wait_ge(*fw)
        ins.then_inc(self.semobj[sbb.dsem], 16)
        sbb.dcnt += 16
        self.dmax[sbb.dsem] = sbb.dcnt
        self._post(sbb.dsem, sbb.dcnt, [in_b], [out_b])

    def barrier(self):
        ev = dict(self.tick)
        ev.update(self.dmax)
        for e in self.eng:
            self._wait(e, {a: b for a, b in ev.items() if b > 0})

    def phase(self):
        kb = self

        class _P:
            def __enter__(s2):
                s2.old = kb.st
                s2.es = ExitStack()
                s2.es.__enter__()
                kb.st = s2.es

            def __exit__(s2, *a):
                if a[0] is None:
                    kb.barrier()
                kb.st = s2.old
                return s2.es.__exit__(*a)
        return _P()

    def finish(self, bufs):
        for b in bufs:
            self._wait("sp", b.w)


def build(cfg, dbg=(), seg=None):
    D, S, DEPTH, NMEM = cfg["D"], cfg["S"], cfg["DEPTH"], cfg["NMEM"]
    TBM, TF = cfg["TBM"], cfg["TF"]
    NA, MIXW, H, MEMW, MHD, DFF, ALPHA = (cfg[k] for k in ("NA", "MIXW", "H", "MEMW", "MHD", "DFF", "ALPHA"))
    NCH, NB = D // 128, S // 128
    NFF, MC, MCH, NMB = DFF // 128, MEMW // 128, MHD // 128, NMEM // 128
    AW, BW = 4 * MIXW + MEMW, MIXW + MEMW
    TX = max(TBM, TF, 128)
    nc = bass.Bass("TRN2", target_bir_lowering=False)

    def din(name, shape, dt=F32):
        return Buf(nc.dram_tensor(name, list(shape), dt, kind="ExternalInput"), dram=True)

    def dsc(name, shape, dt=F32, out=False):
        return Buf(nc.dram_tensor(name, list(shape), dt, kind="ExternalOutput" if (out or name in dbg) else "Internal"), dram=True)

    def need(*segs):
        return seg is None or seg in segs

    LA = NA if seg is None else 1
    LB = DEPTH - NA if seg is None else 1
    LD = DEPTH if seg is None else 1
    x_in = din("x", [S, D]) if need("A", "KV", "B") else None
    mem_in = din("mem", [NMEM, D]) if need("A", "B") else None
    a_w_in = din("a_w_in", [LA, D, AW]) if need("A") else None
    lbl = din("hgrn_lb_logits", [NA, MIXW]) if need("A") else None
    ong = din("a_onorm_g", [LA, MIXW]) if need("A") else None
    lbflag = din("lbflag", [128, 1]) if seg == "A" else None
    b_w_in = din("b_w_in", [LB, D, BW]) if need("B") else None
    w_kv = din("w_kv_shared", [D, 2 * MIXW]) if need("KV") else None
    w_mkv = din("w_mem_kv", [LD, D, 2 * MEMW]) if need("A", "B") else None
    w_o = din("w_o", [LD, D, D]) if need("A", "B") else None
    f_w_in = din("ffn_w_in", [LD, D, 2 * DFF]) if need("FFN") else None
    f_cw = din("ffn_conv_w", [LD, 3, DFF]) if need("FFN") else None
    f_cb = din("ffn_conv_b", [LD, DFF]) if need("FFN") else None
    f_w_out = din("ffn_w_out", [LD, DFF, D]) if need("FFN") else None
    ln_g = din("ln_g", [LD, 2, D]) if need("A", "B", "FFN") else None
    ln_b = din("ln_b", [LD, 2, D]) if need("A", "B", "FFN") else None
    c_ident = din("c_ident", [128, 128], BF16); c_mle = din("c_mle", [128, 128], I32)
    c_nU = din("c_nU", [128, 128]); c_nL = din("c_nL", [128, 128])
    c_m01 = din("c_m01", [TBM // 128, 128, TBM]); c_mb = din("c_mb", [TBM // 128, 128, TBM])
    if seg is None:
        y_out = dsc("y", [S, D], out=True)
        xs = [dsc("xs0", [S, D]), dsc("xs1", [S, D])]
        x1 = dsc("x1s", [S, D]); yp = dsc("yps", [S, D])
        KT = dsc("KT", [H, 128, S], BF16); Vd = dsc("Vd", [S, MIXW], BF16)
    else:
        y_out = None; xs = []
        x1 = dsc("x1s", [S, D], out=True) if seg in ("A", "B") else (din("x1s", [S, D]) if seg == "FFN" else None)
        yp = dsc("yps", [S, D]) if seg == "FFN" else None
        xo = dsc("xo", [S, D], out=True) if seg == "FFN" else None
        if seg == "KV":
            KT = dsc("KT", [H, 128, S], BF16, out=True); Vd = dsc("Vd", [S, MIXW], BF16, out=True)
        elif seg == "B":
            KT = din("KT", [H, 128, S], BF16); Vd = din("Vd", [S, MIXW], BF16)
        else:
            KT = Vd = None

    st = ExitStack()
    with st:
        k = KB(nc, st)
        sb, ps = k.sb, k.ps

        def wl(layer):
            return layer if seg is None else 0
        ident = sb([128, 128], BF16); mle = sb([128, 128], I32); nU = sb([128, 128], F32); nL = sb([128, 128], F32)
        m01 = sb([128, TBM // 128, TBM], F32); mb = sb([128, TBM // 128, TBM], F32)
        ones = sb([128, 128], F32); zcol = sb([128, 1], F32)
        k.dma("sp", ident, ident.t[:], c_ident, c_ident.t[:, :])
        k.dma("sp", mle, mle.t[:], c_mle, c_mle.t[:, :])
        k.dma("sp", nU, nU.t[:], c_nU, c_nU.t[:, :])
        k.dma("sp", nL, nL.t[:], c_nL, c_nL.t[:, :])
        k.dma("sp", m01, m01.t[:], c_m01, c_m01.t.ap().rearrange("o p t -> p o t"))
        k.dma("sp", mb, mb.t[:], c_mb, c_mb.t.ap().rearrange("o p t -> p o t"))
        k.op("dve", "memset", [], [ones], ones.t[:], 1.0)
        k.op("dve", "memset", [], [zcol], zcol.t[:], 0.0)
        CG = 256
        WSH = {"ws": None, "cgt": 256}
        wsi = [0]

        def alloc_ws(cgt):
            WSH["ws"] = [sb([128, NCH, cgt], BF16, "ws") for _ in range(2)]
            WSH["cgt"] = cgt
        xT = None
        xld = sb([128, D], F32, "xld"); ysb = sb([128, D], F32, "ysb"); xbf = sb([128, D], BF16, "xbf")
        SEG = 1024 if D % 1024 == 0 else D
        gbc = sb([128, SEG], F32, "gbc"); bbc = sb([128, SEG], F32, "bbc")
        st1 = sb([128, 8], F32, "st1")
        cwT = sb([128, 3, NFF], F32, "cwT"); cbT = sb([128, NFF], F32, "cbT")
        PS = [ps([128, 512], F32, "ps") for _ in range(6)]
        PT = [ps([128, 1024], BF16, "pt") for _ in range(2)]
        psi = [0]; pti = [0]
        pcum, pacc = PS[4], PS[5]

        def nps():
            psi[0] = (psi[0] + 1) % 4; return PS[psi[0]]

        def npt():
            pti[0] = (pti[0] + 1) % len(PT); return PT[pti[0]]

        def wslab(wb, wap_rows, k0, kn, c0, ncol):
            wsi[0] ^= 1
            s = WSH["ws"][wsi[0]]
            src = wap_rows[k0 * 128:(k0 + kn) * 128, c0:c0 + ncol].rearrange("(c p) m -> p c m", p=128)
            k.dma("pool", s, s.t[:, 0:kn, 0:ncol], wb, src)
            return s

        def transp(src_b, src_fn, n0, n1, dst_b, dst_fn, eng="dve"):
            for c4 in range(n0, n1, 8):
                pt = npt(); n = min(8, n1 - c4)
                for j in range(n):
                    k.op("pe", "transpose", [src_b, ident], [pt], pt.t[:, j * 128:(j + 1) * 128], src_fn(c4 + j), ident.t[:])
                k.op(eng, "tensor_copy" if eng == "dve" else "copy", [pt], [dst_b], dst_fn(c4, n), pt.t[:, 0:n * 128].rearrange("p (c t) -> p c t", c=n))

        def load_xT(src, r0, T):
            for tb in range(T // 128):
                k.dma("sp", xld, xld.t[:], src, src.t[r0 + tb * 128:r0 + (tb + 1) * 128, :])
                k.op("act", "copy", [xld], [xbf], xbf.t[:], xld.t[:])
                transp(xbf, lambda c: xbf.t[:, c * 128:(c + 1) * 128], 0, NCH, xT, lambda c0, n: xT.t[:, c0:c0 + n, tb * 128:(tb + 1) * 128])

        def gemm_feat(s, j, nk, rhs_b, rhs_fn, N):
            p = nps()
            for kk in range(nk):
                k.op("pe", "matmul", [s, rhs_b], [p], p.t[:, 0:N], s.t[:, kk, j * 128:(j + 1) * 128], rhs_fn(kk), start=(kk == 0), stop=(kk == nk - 1))
            return p

        def tok_gemm(s, nco, tb, out_fn):
            p = nps()
            for kk in range(NCH):
                k.op("pe", "matmul", [xT, s], [p], p.t[:, 0:nco], xT.t[:, kk, tb * 128:(tb + 1) * 128], s.t[:, kk, 0:nco], start=(kk == 0), stop=(kk == NCH - 1))
            out_fn(p)

        def proj_ln(layer, xres, r0, dst, wbuf, wrows, k_lo, k_hi, lhs_b, lhs_fn, gi, mode="full"):
            CGT = WSH["cgt"]
            k.dma("sp", xld, xld.t[:], xres, xres.t[r0:r0 + 128, :])
            for cg in range(D // CGT):
                p = nps()
                for k0 in range(k_lo, k_hi, NCH):
                    kn = min(NCH, k_hi - k0)
                    s = wslab(wbuf, wrows, k0, kn, cg * CGT, CGT)
                    for kk in range(kn):
                        k.op("pe", "matmul", [lhs_b, s], [p], p.t[:, 0:CGT], lhs_fn(k0 + kk), s.t[:, kk, 0:CGT], start=(k0 + kk == k_lo), stop=(k0 + kk == k_hi - 1))
                k.op("dve", "scalar_tensor_tensor", [xld, p], [ysb], out=ysb.t[:, cg * CGT:(cg + 1) * CGT], in0=xld.t[:, cg * CGT:(cg + 1) * CGT],
                     scalar=(1.0 if mode == "last" else float(ALPHA)), in1=p.t[:, 0:CGT], op0=OP.mult, op1=OP.add)
            if mode != "first":
                k.op("dve", "reduce_sum", [ysb], [st1], out=st1.t[:, 0:1], in_=ysb.t[:], axis=AX.X)
                k.op("dve", "tensor_scalar", [st1], [st1], out=st1.t[:, 1:2], in0=st1.t[:, 0:1], scalar1=-1.0 / D, scalar2=None, op0=OP.mult)
                k.op("act", "activation", [ysb, st1], [ysb], out=ysb.t[:], in_=ysb.t[:], func=AF.Identity, bias=st1.t[:, 1:2], scale=1.0)
                k.op("act", "activation", [ysb], [xld, st1], out=xld.t[:], in_=ysb.t[:], func=AF.Square, accum_out=st1.t[:, 2:3])
                k.op("dve", "tensor_scalar", [st1], [st1], out=st1.t[:, 3:4], in0=st1.t[:, 2:3], scalar1=1.0 / D, scalar2=1e-5, op0=OP.mult, op1=OP.add)
                k.op("act", "activation", [st1], [st1], out=st1.t[:, 4:5], in_=st1.t[:, 3:4], func=AF.Ln)
                k.op("act", "activation", [st1], [st1], out=st1.t[:, 4:5], in_=st1.t[:, 4:5], func=AF.Exp, scale=-0.5)
                for sg in range(D // SEG):
                    sl = slice(sg * SEG, (sg + 1) * SEG)
                    k.dma("sp", gbc, gbc.t[:], ln_g, ln_g.t[wl(layer), gi:gi + 1, sl].partition_broadcast(128))
                    k.dma("sp", bbc, bbc.t[:], ln_b, ln_b.t[wl(layer), gi:gi + 1, sl].partition_broadcast(128))
                    k.op("dve", "scalar_tensor_tensor", [ysb, st1, gbc], [ysb], out=ysb.t[:, sl], in0=ysb.t[:, sl], scalar=st1.t[:, 4:5], in1=gbc.t[:], op0=OP.mult, op1=OP.mult)
                    k.op("dve", "tensor_tensor", [ysb, bbc], [ysb], out=ysb.t[:, sl], in0=ysb.t[:, sl], in1=bbc.t[:], op=OP.add)
            k.dma("sp", dst, dst.t[r0:r0 + 128, :], ysb, ysb.t[:])

        def mixer_phase(layer, cur):
            nonlocal xT
            TM = 128 if layer < NA else TBM
            alloc_ws(256)
            xT = sb([128, NCH, TM], BF16, "xT")
            mkT = sb([128, MC, NMEM], BF16, "mkT"); mv = sb([128, NMB, MEMW], BF16, "mv")
            mqT = sb([128, MC, TM], BF16, "mqT"); ocT = sb([128, NCH, TM], BF16, "ocT"); otok = sb([128, D], BF16, "otok")
            pmem = sb([128, NMEM], BF16, "pmem"); pTm = sb([128, NMB, 128], BF16, "pTm")
            with k.phase():
                memT = sb([128, NCH, NMEM], BF16, "memT")
                for mbk in range(NMB):
                    k.dma("sp", xld, xld.t[:], mem_in, mem_in.t[mbk * 128:(mbk + 1) * 128, :])
                    k.op("act", "copy", [xld], [xbf], xbf.t[:], xld.t[:])
                    transp(xbf, lambda c: xbf.t[:, c * 128:(c + 1) * 128], 0, NCH, memT, lambda c0, n: memT.t[:, c0:c0 + n, mbk * 128:(mbk + 1) * 128])
                wr = w_mkv.t[wl(layer)]
                for c0 in range(0, MEMW, CG):
                    nco = min(CG, MEMW - c0)
                    s = wslab(w_mkv, wr, 0, NCH, c0, nco)
                    for j in range(nco // 128):
                        p = gemm_feat(s, j, NCH, memT, lambda kk: memT.t[:, kk, :], NMEM)
                        k.op("act", "copy", [p], [mkT], mkT.t[:, c0 // 128 + j, :], p.t[:, 0:NMEM])
                for c0 in range(0, MEMW, CG):
                    nco = min(CG, MEMW - c0)
                    s = wslab(w_mkv, wr, 0, NCH, MEMW + c0, nco)
                    for mbk in range(NMB):
                        p = nps()
                        for kk in range(NCH):
                            k.op("pe", "matmul", [memT, s], [p], p.t[:, 0:nco], memT.t[:, kk, mbk * 128:(mbk + 1) * 128], s.t[:, kk, 0:nco], start=(kk == 0), stop=(kk == NCH - 1))
                        k.op("act", "copy", [p], [mv], mv.t[:, mbk, c0:c0 + nco], p.t[:, 0:nco])

            def mem_attn(tb):
                sc = float(MHD ** -0.5)
                for h in range(4):
                    p = nps()
                    for c in range(MCH):
                        k.op("pe", "matmul", [mqT, mkT], [p], p.t[:, 0:NMEM], mqT.t[:, h * MCH + c, tb * 128:(tb + 1) * 128], mkT.t[:, h * MCH + c, :], start=(c == 0), stop=(c == MCH - 1))
                    k.op("dve", "reduce_max", [p], [st1], out=st1.t[:, 5:6], in_=p.t[:, 0:NMEM], axis=AX.X)
                    k.op("dve", "tensor_scalar", [st1], [st1], out=st1.t[:, 6:7], in0=st1.t[:, 5:6], scalar1=-sc, scalar2=None, op0=OP.mult)
                    k.op("act", "activation", [p, st1], [pmem, st1], out=pmem.t[:], in_=p.t[:, 0:NMEM], func=AF.Exp, bias=st1.t[:, 6:7], scale=sc, accum_out=st1.t[:, 7:8])
                    transp(pmem, lambda m: pmem.t[:, m * 128:(m + 1) * 128], 0, NMB, pTm, lambda c0, n: pTm.t[:, c0:c0 + n, :], eng="act")
                    p2 = nps()
                    for m in range(NMB):
                        k.op("pe", "matmul", [pTm, mv], [p2], p2.t[:, 0:MHD], pTm.t[:, m, :], mv.t[:, m, h * MHD:(h + 1) * MHD], start=(m == 0), stop=(m == NMB - 1))
                    k.op("dve", "reciprocal", [st1], [st1], out=st1.t[:, 5:6], in_=st1.t[:, 7:8])
                    k.op("dve", "tensor_scalar", [p2, st1], [otok], out=otok.t[:, MIXW + h * MHD:MIXW + (h + 1) * MHD], in0=p2.t[:, 0:MHD], scalar1=st1.t[:, 5:6], scalar2=None, op0=OP.mult)

            def finish_block(r0, tb, c_lo, c_hi):
                mem_attn(tb)
                transp(otok, lambda c: otok.t[:, c * 128:(c + 1) * 128], c_lo, c_hi, ocT, lambda c0, n: ocT.t[:, c0:c0 + n, tb * 128:(tb + 1) * 128], eng="act")
                proj_ln(layer, cur, r0 + tb * 128, x1, w_o, w_o.t[wl(layer)], 0, NCH, ocT, lambda kk: ocT.t[:, kk, tb * 128:(tb + 1) * 128], 0)

            if layer < NA:
                a_mixer(layer, cur, mqT, otok, finish_block)
            else:
                b_mixer(layer, cur, mqT, ocT, finish_block)

        def a_mixer(layer, cur, mqT, otok, finish_block):
            TT = 128
            Sst = sb([128, H, 128], F32, "Sst"); Sbf = sb([128, H, 128], BF16, "Sbf")
            lbT = sb([128, H], F32, "lbT"); omlT = sb([128, H], F32, "omlT"); lb2 = sb([128, 2, H], F32, "lb2")
            ongb = sb([128, CG], F32, "ongb")
            vtok = sb([128, MIXW], BF16, "vtok"); gg = sb([128, MIXW], F32, "gg")
            qs = sb([128, TT], F32, "qs"); fk = sb([128, TT], F32, "fk"); kk_ = sb([128, TT], F32, "kk"); Bc = sb([128, TT], F32, "Bc"); nB = sb([128, TT], F32, "nB")
            E = sb([128, 128], F32, "E"); qt_ = sb([128, 128], BF16, "qt"); kt_ = sb([128, 128], BF16, "kt"); qh = sb([128, 128], BF16, "qh"); kh = sb([128, 128], BF16, "kh")
            khT = sb([128, 128], BF16, "khT"); Pm = sb([128, 128], BF16, "Pm"); dec = sb([128, 4], F32, "dec"); osq = sb([128, 128], F32, "osq")
            for l_ in range(2):
                k.dma("sp", lb2, lb2.t[:, l_, :], lbl, lbl.t[l_, :].rearrange("(h p) -> p h", p=128), allow_slow_non_contiguous=True)
            if seg is None and layer == 0:
                k.op("dve", "memset", [], [lbT], lbT.t[:], 0.0)
            else:
                k.op("dve", "tensor_tensor", [lb2], [lbT], out=lbT.t[:], in0=lb2.t[:, 1, :], in1=lb2.t[:, 0, :], op=OP.subtract)
                k.op("act", "activation", [lbT], [lbT], out=lbT.t[:], in_=lbT.t[:], func=AF.Sigmoid)
                if seg is not None:
                    flg = sb([128, 1], F32, "flg")
                    k.dma("sp", flg, flg.t[:], lbflag, lbflag.t[:, :])
                    k.op("dve", "tensor_scalar", [lbT, flg], [lbT], out=lbT.t[:], in0=lbT.t[:], scalar1=flg.t[:, 0:1], scalar2=None, op0=OP.mult)
            k.op("dve", "tensor_scalar", [lbT], [omlT], out=omlT.t[:], in0=lbT.t[:], scalar1=-1.0, scalar2=1.0, op0=OP.mult, op1=OP.add)
            k.op("dve", "memset", [], [Sst], Sst.t[:], 0.0)
            k.op("dve", "memset", [], [Sbf], Sbf.t[:], 0.0)
            wr = a_w_in.t[wl(layer)]
            xTa = lambda kk: xT.t[:, kk, 0:TT]
            for ti in range(S // TT):
                r0 = ti * TT
                load_xT(cur, r0, TT)
                for c0 in range(0, MIXW, CG):
                    nco = min(CG, MIXW - c0)
                    s = wslab(a_w_in, wr, 0, NCH, 2 * MIXW + c0, nco)
                    tok_gemm(s, nco, 0, lambda p: k.op("act", "copy", [p], [vtok], vtok.t[:, c0:c0 + nco], p.t[:, 0:nco]))
                    s = wslab(a_w_in, wr, 0, NCH, 3 * MIXW + c0, nco)
                    k.dma("sp", ongb, ongb.t[:, 0:nco], ong, ong.t[wl(layer):wl(layer) + 1, c0:c0 + nco].partition_broadcast(128))

                    def gout(p):
                        k.op("act", "activation", [p], [gg], out=gg.t[:, c0:c0 + nco], in_=p.t[:, 0:nco], func=AF.Silu)
                        k.op("dve", "tensor_tensor", [gg, ongb], [gg], out=gg.t[:, c0:c0 + nco], in0=gg.t[:, c0:c0 + nco], in1=ongb.t[:, 0:nco], op=OP.mult)
                    tok_gemm(s, nco, 0, gout)
                for c0 in range(0, MEMW, CG):
                    nco = min(CG, MEMW - c0)
                    s = wslab(a_w_in, wr, 0, NCH, 4 * MIXW + c0, nco)
                    for j in range(nco // 128):
                        p = gemm_feat(s, j, NCH, xT, xTa, TT)
                        k.op("act", "copy", [p], [mqT], mqT.t[:, c0 // 128 + j, :], p.t[:, 0:TT])
                HG = CG // 128
                for h4 in range(0, H, HG):
                    nh = min(HG, H - h4)
                    sq = wslab(a_w_in, wr, 0, NCH, h4 * 128, nh * 128)
                    sf = wslab(a_w_in, wr, 0, NCH, MIXW + h4 * 128, nh * 128)
                    for j in range(nh):
                        h = h4 + j
                        pq = gemm_feat(sq, j, NCH, xT, xTa, TT)
                        k.op("act", "activation", [pq], [qs], out=qs.t[:], in_=pq.t[:, 0:TT], func=AF.Silu)
                        pf = gemm_feat(sf, j, NCH, xT, xTa, TT)
                        k.op("act", "activation", [pf], [fk], out=fk.t[:], in_=pf.t[:, 0:TT], func=AF.Sigmoid)
                        k.op("dve", "tensor_scalar", [fk, omlT, lbT], [fk], out=fk.t[:], in0=fk.t[:], scalar1=omlT.t[:, h:h + 1], scalar2=lbT.t[:, h:h + 1], op0=OP.mult, op1=OP.add)
                        k.op("dve", "tensor_scalar", [fk], [kk_], out=kk_.t[:], in0=fk.t[:], scalar1=-1.0, scalar2=1.0, op0=OP.mult, op1=OP.add)
                        k.op("act", "activation", [fk], [fk], out=fk.t[:], in_=fk.t[:], func=AF.Ln)
                        k.op("dve", "tensor_tensor_scan", [ones, fk], [Bc], out=Bc.t[:], data0=ones.t[:], data1=fk.t[:], initial=0.0, op0=OP.mult, op1=OP.add)
                        k.op("dve", "tensor_scalar", [Bc], [nB], out=nB.t[:], in0=Bc.t[:], scalar1=-1.0, scalar2=None, op0=OP.mult)
                        rb = [Bc, nB]
                        k.op("act", "activation", rb, [E], out=E.t[:], in_=Bc.t[:], func=AF.Exp, bias=nB.t[:, 63:64], scale=1.0)
                        k.op("dve", "tensor_tensor", [qs, E], [qt_], out=qt_.t[:], in0=qs.t[:], in1=E.t[:], op=OP.mult)
                        k.op("act", "activation", rb, [E], out=E.t[:], in_=Bc.t[:], func=AF.Exp, bias=Bc.t[:, 63:64], scale=-1.0)
                        k.op("dve", "tensor_tensor", [kk_, E], [kt_], out=kt_.t[:], in0=kk_.t[:], in1=E.t[:], op=OP.mult)
                        k.op("act", "activation", rb, [E], out=E.t[:], in_=Bc.t[:], func=AF.Exp)
                        k.op("dve", "tensor_tensor", [qs, E], [qh], out=qh.t[:], in0=qs.t[:], in1=E.t[:], op=OP.mult)
                        k.op("act", "activation", rb, [E], out=E.t[:], in_=Bc.t[:], func=AF.Exp, bias=Bc.t[:, 127:128], scale=-1.0)
                        k.op("dve", "tensor_tensor", [kk_, E], [kh], out=kh.t[:], in0=kk_.t[:], in1=E.t[:], op=OP.mult)
                        k.op("act", "activation", rb, [dec], out=dec.t[:, 0:1], in_=Bc.t[:, 127:128], func=AF.Exp)
                        p = nps()
                        k.op("pe", "matmul", [kt_, qt_], [p], p.t[:, 0:128], kt_.t[:], qt_.t[:], start=True, stop=True)
                        k.op("dve", "memset", [], [Pm], Pm.t[:], 0.0)
                        k.op("dve", "copy_predicated", [p, mle], [Pm], out=Pm.t[:], mask=mle.t[:], data=p.t[:, 0:128])
                        pt = npt()
                        k.op("pe", "transpose", [kh, ident], [pt], pt.t[:, 0:128], kh.t[:], ident.t[:])
                        k.op("act", "copy", [pt], [khT], khT.t[:], pt.t[:, 0:128])
                        po = nps()
                        k.op("pe", "matmul", [Pm, vtok], [po], po.t[:, 0:128], Pm.t[:], vtok.t[:, h * 128:(h + 1) * 128], start=True, stop=False)
                        k.op("pe", "matmul", [qh, Sbf], [po], po.t[:, 0:128], qh.t[:], Sbf.t[:, h, :], start=False, stop=True)
                        pd = nps()
                        k.op("pe", "matmul", [khT, vtok], [pd], pd.t[:, 0:128], khT.t[:], vtok.t[:, h * 128:(h + 1) * 128], start=True, stop=True)
                        k.op("dve", "scalar_tensor_tensor", [Sst, dec, pd], [Sst], out=Sst.t[:, h, :], in0=Sst.t[:, h, :], scalar=dec.t[:, 0:1], in1=pd.t[:, 0:128], op0=OP.mult, op1=OP.add)
                        k.op("act", "copy", [Sst], [Sbf], Sbf.t[:, h, :], Sst.t[:, h, :])
                        k.op("act", "activation", [po], [osq, dec], out=osq.t[:], in_=po.t[:, 0:128], func=AF.Square, accum_out=dec.t[:, 1:2])
                        k.op("dve", "tensor_scalar", [dec], [dec], out=dec.t[:, 2:3], in0=dec.t[:, 1:2], scalar1=1.0 / 128, scalar2=1e-6, op0=OP.mult, op1=OP.add)
                        k.op("act", "activation", [dec], [dec], out=dec.t[:, 3:4], in_=dec.t[:, 2:3], func=AF.Ln)
                        k.op("act", "activation", [dec], [dec], out=dec.t[:, 3:4], in_=dec.t[:, 3:4], func=AF.Exp, scale=-0.5)
                        k.op("dve", "scalar_tensor_tensor", [po, dec, gg], [otok], out=otok.t[:, h * 128:(h + 1) * 128], in0=po.t[:, 0:128], scalar=dec.t[:, 3:4],
                             in1=gg.t[:, h * 128:(h + 1) * 128], op0=OP.mult, op1=OP.mult)
                finish_block(r0, 0, 0, NCH)

        def kv_body(cur, wsb):
            TT = TBM; TBB = TT // 128
            xTa = lambda kk: xT.t[:, kk, 0:TT]
            with k.phase():
                    vtok = sb([128, MIXW], BF16, "vtokb")
                    for ti in range(S // TT):
                        r0 = ti * TT
                        load_xT(cur, r0, TT)
                        for c0 in range(0, MIXW, CG):
                            nco = min(CG, MIXW - c0)
                            s = wslab(w_kv, w_kv.t, 0, NCH, c0, nco)
                            for j in range(nco // 128):
                                h = c0 // 128 + j
                                p = gemm_feat(s, j, NCH, xT, xTa, TT)
                                k.op("act", "copy", [p], [wsb], wsb.t[:], p.t[:, 0:TT])
                                k.dma("sp", KT, KT.t[h, :, r0:r0 + TT], wsb, wsb.t[:])
                        for tb in range(TBB):
                            for c0 in range(0, MIXW, CG):
                                nco = min(CG, MIXW - c0)
                                s = wslab(w_kv, w_kv.t, 0, NCH, MIXW + c0, nco)
                                tok_gemm(s, nco, tb, lambda p: k.op("act", "copy", [p], [vtok], vtok.t[:, c0:c0 + nco], p.t[:, 0:nco]))
                            k.dma("sp", Vd, Vd.t[r0 + tb * 128:r0 + (tb + 1) * 128, :], vtok, vtok.t[:])

        def kv_phase(cur):
            nonlocal xT
            alloc_ws(256)
            xT = sb([128, NCH, TBM], BF16, "xT")
            wsb = sb([128, TBM], BF16, "wsbk")
            kv_body(cur, wsb)

        def b_mixer(layer, cur, mqT, ocT, finish_block):
            TT = TBM; TBB = TT // 128; GK = 8
            li = layer - NA
            qTb = sb([128, H, TT], BF16, "qTb")
            KTh = sb([128, GK * 128], BF16, "KTh"); Vh = sb([128, GK, 128], BF16, "Vh")
            esb = sb([128, TT], F32, "esb"); spb = sb([128, TT], F32, "spb"); t1 = sb([128, TT], F32, "t1"); wsb = sb([128, TT], BF16, "wsb")
            xTa = lambda kk: xT.t[:, kk, 0:TT]
            if seg is None and layer == NA:
                kv_body(cur, wsb)
            wr = b_w_in.t[li if seg is None else 0]
            sc = float(128 ** -0.5)
            for ti in range(S // TT):
                r0 = ti * TT
                load_xT(cur, r0, TT)
                for c0 in range(0, BW, CG):
                    nco = min(CG, BW - c0)
                    s = wslab(b_w_in, wr, 0, NCH, c0, nco)
                    for j in range(nco // 128):
                        cc = c0 // 128 + j
                        p = gemm_feat(s, j, NCH, xT, xTa, TT)
                        if cc < H:
                            k.op("act", "copy", [p], [qTb], qTb.t[:, cc, :], p.t[:, 0:TT])
                        else:
                            k.op("act", "copy", [p], [mqT], mqT.t[:, cc - H, :], p.t[:, 0:TT])
                kb0 = ti * TBB
                kbmax = kb0 + TBB - 1
                for h in range(H):
                    for kb in range(kbmax, -1, -1):
                        g0 = (kb // GK) * GK
                        if kb == kbmax or kb % GK == GK - 1:
                            ng = kb - g0 + 1
                            k.dma("sp", KTh, KTh.t[:, 0:ng * 128], KT, KT.t[h, :, g0 * 128:(g0 + ng) * 128])
                            k.dma("sp", Vh, Vh.t[:, 0:ng, :], Vd, Vd.t[g0 * 128:(g0 + ng) * 128, h * 128:(h + 1) * 128].rearrange("(b p) d -> p b d", p=128))
                        kl = kb - g0
                        o = kb - kb0
                        pz = nps()
                        k.op("pe", "matmul", [KTh, qTb], [pz], pz.t[:, 0:TT], KTh.t[:, kl * 128:(kl + 1) * 128], qTb.t[:, h, :], start=True, stop=True)
                        k.op("act", "activation", [pz], [esb], out=esb.t[:], in_=pz.t[:, 0:TT], func=AF.Exp, scale=sc)
                        k.op("act", "activation", [esb], [spb], out=spb.t[:], in_=esb.t[:], func=AF.Ln, bias=1.0, scale=1.0)
                        k.op("dve", "scalar_tensor_tensor", [pz, spb], [t1], out=t1.t[:], in0=pz.t[:, 0:TT], scalar=sc, in1=spb.t[:], op0=OP.mult, op1=OP.subtract)
                        if o >= 0:
                            k.op("dve", "tensor_tensor", [spb, m01], [spb], out=spb.t[:], in0=spb.t[:], in1=m01.t[:, o, :], op=OP.mult)
                        k.op("pe", "matmul", [nU, spb], [pcum], pcum.t[:, 0:TT], nU.t[:], spb.t[:], start=(kb == kbmax), stop=False, skip_group_check=True)
                        k.op("dve", "tensor_tensor", [t1, pcum], [t1], out=t1.t[:], in0=t1.t[:], in1=pcum.t[:, 0:TT], op=OP.add)
                        if o >= 0:
                            k.op("dve", "tensor_tensor", [t1, mb], [t1], out=t1.t[:], in0=t1.t[:], in1=mb.t[:, o, :], op=OP.add)
                        k.op("pe", "matmul", [nL, spb], [pcum], pcum.t[:, 0:TT], nL.t[:], spb.t[:], start=False, stop=(kb == 0), skip_group_check=True)
                        k.op("act", "activation", [t1], [wsb], out=wsb.t[:], in_=t1.t[:], func=AF.Exp)
                        k.op("pe", "matmul", [Vh, wsb], [pacc], pacc.t[:, 0:TT], Vh.t[:, kl, :], wsb.t[:], start=(kb == kbmax), stop=(kb == 0), skip_group_check=True)
                    k.op("act", "copy", [pacc], [ocT], ocT.t[:, h, :], pacc.t[:, 0:TT])
                for tb in range(TBB):
                    finish_block(r0, tb, H, H + MC)

        def ffn_phase(layer, dst):
            nonlocal xT
            TT = TF
            alloc_ws(512)
            xT = sb([128, NCH, TT], BF16, "xT")
            NH = (NFF + 1) // 2 if NFF > 8 else NFF
            halves = [(0, NH)] + ([(NH, NFF)] if NH < NFF else [])
            hT = sb([128, NH, TT], BF16, "hT")
            gcar = sb([128, NFF, 2], F32, "gcar")
            gsb = sb([128, TT + 2], F32, "gsb"); csb = sb([128, TT], F32, "csb")
            xTa = lambda kk: xT.t[:, kk, 0:TT]
            for f0 in range(0, NFF, 16):
                f1 = min(NFF, f0 + 16)
                for tap in range(3):
                    k.dma("sp", cwT, cwT.t[:, tap, f0:f1], f_cw, f_cw.t[wl(layer), tap, f0 * 128:f1 * 128].rearrange("(f p) -> p f", p=128), allow_slow_non_contiguous=True)
                k.dma("sp", cbT, cbT.t[:, f0:f1], f_cb, f_cb.t[wl(layer), f0 * 128:f1 * 128].rearrange("(f p) -> p f", p=128), allow_slow_non_contiguous=True)
            k.op("dve", "memset", [], [gcar], gcar.t[:], 0.0)
            wr = f_w_in.t[wl(layer)]
            for ti in range(S // TT):
                r0 = ti * TT
                load_xT(x1, r0, TT)
                for hi, (fa, fb) in enumerate(halves):
                    for c0 in range(fa * 128, fb * 128, CG):
                        nco = min(CG, fb * 128 - c0)
                        sg = wslab(f_w_in, wr, 0, NCH, c0, nco)
                        su = wslab(f_w_in, wr, 0, NCH, DFF + c0, nco)
                        for j in range(nco // 128):
                            f = c0 // 128 + j
                            pg = gemm_feat(sg, j, NCH, xT, xTa, TT)
                            k.op("act", "copy", [gcar], [gsb], gsb.t[:, 0:2], gcar.t[:, f, :])
                            k.op("act", "copy", [pg], [gsb], gsb.t[:, 2:2 + TT], pg.t[:, 0:TT])
                            k.op("act", "copy", [gsb], [gcar], gcar.t[:, f, :], gsb.t[:, TT:TT + 2])
                            k.op("dve", "tensor_scalar", [gsb, cwT, cbT], [csb], out=csb.t[:], in0=gsb.t[:, 0:TT], scalar1=cwT.t[:, 0, f:f + 1], scalar2=cbT.t[:, f:f + 1], op0=OP.mult, op1=OP.add)
                            k.op("dve", "scalar_tensor_tensor", [gsb, cwT, csb], [csb], out=csb.t[:], in0=gsb.t[:, 1:1 + TT], scalar=cwT.t[:, 1, f:f + 1], in1=csb.t[:], op0=OP.mult, op1=OP.add)
                            k.op("dve", "scalar_tensor_tensor", [gsb, cwT, csb], [csb], out=csb.t[:], in0=gsb.t[:, 2:2 + TT], scalar=cwT.t[:, 2, f:f + 1], in1=csb.t[:], op0=OP.mult, op1=OP.add)
                            k.op("act", "activation", [csb], [csb], out=csb.t[:], in_=csb.t[:], func=AF.Silu)
                            pu = gemm_feat(su, j, NCH, xT, xTa, TT)
                            k.op("dve", "tensor_tensor", [csb, pu], [hT], out=hT.t[:, f - fa, :], in0=csb.t[:], in1=pu.t[:, 0:TT], op=OP.mult)
                    mode = "full" if len(halves) == 1 else ("first" if hi == 0 else "last")
                    for tb in range(TT // 128):
                        proj_ln(layer, yp if mode == "last" else x1, r0 + tb * 128, yp if mode == "first" else dst, f_w_out, f_w_out.t[wl(layer)], fa, fb,
                                hT, lambda kk: hT.t[:, kk - fa, tb * 128:(tb + 1) * 128], 1, mode=mode)

        if seg is None:
            cur = x_in
            for layer in range(DEPTH):
                dst = y_out if layer == DEPTH - 1 else xs[layer % 2]
                with k.phase():
                    mixer_phase(layer, cur)
                with k.phase():
                    ffn_phase(layer, dst)
                cur = dst
            k.finish([y_out] + [b for b in (xs + [x1, KT, Vd]) if b.t.name in dbg])
        elif seg == "A":
            with k.phase():
                mixer_phase(0, x_in)
            k.finish([x1])
        elif seg == "KV":
            with k.phase():
                kv_phase(x_in)
            k.finish([KT, Vd])
        elif seg == "B":
            with k.phase():
                mixer_phase(NA + 1, x_in)
            k.finish([x1])
        elif seg == "FFN":
            with k.phase():
                ffn_phase(0, xo)
            k.finish([xo])
    return nc


def _consts(cfg):
    TT = cfg["TBM"]; TB = TT // 128
    s = np.arange(128)[:, None]; t = np.arange(128)[None, :]
    c = {}
    c["c_ident"] = np.eye(128, dtype=np.float32).astype(ml_dtypes.bfloat16)
    c["c_mle"] = (s <= t).astype(np.int32)
    U = (s > t).astype(np.float32)
    c["c_nU"] = -U
    c["c_nL"] = -(1.0 - U)
    tt = np.arange(TT)[None, :]
    m = np.stack([((s + 128 * o) < tt) for o in range(TB)]).astype(np.float32)
    c["c_m01"] = m
    c["c_mb"] = (m - 1.0) * 30000.0
    return c


def run(cfg, inputs, dbg=()):
    nc = build(cfg, dbg)
    im = {}
    for kname, v in inputs.items():
        a = np.ascontiguousarray(np.asarray(v, dtype=np.float32))
        if kname in ("x", "mem"):
            a = a.reshape(a.shape[-2], a.shape[-1])
        im[kname] = a
    im.update(_consts(cfg))
    res = run_bass_kernel_spmd(nc, [im], core_ids=[0])
    return res.results[0]


def run_segments(cfg, inputs):
    f32 = lambda v: np.ascontiguousarray(np.asarray(v, dtype=np.float32))
    W = {kname: f32(v) for kname, v in inputs.items()}
    S, D, NA, DEPTH = cfg["S"], cfg["D"], cfg["NA"], cfg["DEPTH"]
    consts = _consts(cfg)
    progs = {}

    def launch(seg, im):
        if seg not in progs:
            progs[seg] = build(cfg, seg=seg)
        im = dict(im)
        im.update(consts)
        return run_bass_kernel_spmd(progs[seg], [im], core_ids=[0]).results[0]

    x = W["x"].reshape(S, D)
    mem = W["mem"].reshape(cfg["NMEM"], D)
    KTa = Vda = None
    for layer in range(DEPTH):
        lw = dict(w_mem_kv=W["w_mem_kv"][layer:layer + 1], w_o=W["w_o"][layer:layer + 1],
                  ln_g=W["ln_g"][layer:layer + 1], ln_b=W["ln_b"][layer:layer + 1])
        if layer < NA:
            flag = np.full((128, 1), 1.0 if layer > 0 else 0.0, np.float32)
            r = launch("A", dict(x=x, mem=mem, a_w_in=W["a_w_in"][layer:layer + 1], hgrn_lb_logits=W["hgrn_lb_logits"],
                                 a_onorm_g=W["a_onorm_g"][layer:layer + 1], lbflag=flag, **lw))
        else:
            if layer == NA:
                r = launch("KV", dict(x=x, w_kv_shared=W["w_kv_shared"]))
                KTa, Vda = r["KT"], r["Vd"]
            r = launch("B", dict(x=x, mem=mem, b_w_in=W["b_w_in"][layer - NA:layer - NA + 1], KT=KTa, Vd=Vda, **lw))
        x1 = r["x1s"]
        r = launch("FFN", dict(x1s=x1, ffn_w_in=W["ffn_w_in"][layer:layer + 1], ffn_conv_w=W["ffn_conv_w"][layer:layer + 1],
                               ffn_conv_b=W["ffn_conv_b"][layer:layer + 1], ffn_w_out=W["ffn_w_out"][layer:layer + 1],
                               ln_g=W["ln_g"][layer:layer + 1], ln_b=W["ln_b"][layer:layer + 1]))
        x = r["xo"]
    return x


def kernel(**inputs):
    cfg = make_cfg()
    out = run_segments(cfg, inputs)
    return np.asarray(out, dtype=np.float32).reshape(1, cfg["S"], cfg["D"])
```

```python
import numpy as np
import ml_dtypes
from contextlib import ExitStack
import concourse.bass as bass
import concourse.mybir as mybir
from concourse.bass_utils import run_bass_kernel_spmd

AF = mybir.ActivationFunctionType
OP = mybir.AluOpType
AX = mybir.AxisListType
F32, BF16, I32 = mybir.dt.float32, mybir.dt.bfloat16, mybir.dt.int32


def make_cfg(D=4096, S=8192, DEPTH=4, NMEM=256, DFF=None, TBM=256, TF=512):
    c = dict(D=D, S=S, DEPTH=DEPTH, NMEM=NMEM, TBM=TBM, TF=TF)
    c["NA"] = DEPTH // 2
    c["MIXW"] = 3 * D // 4
    c["H"] = c["MIXW"] // 128
    c["MEMW"] = D - c["MIXW"]
    c["MHD"] = c["MEMW"] // 4
    c["DFF"] = DFF if DFF is not None else 256 * ((8 * D // 3 + 255) // 256)
    c["ALPHA"] = (2 * DEPTH) ** 0.25
    return c


class Buf:
    def __init__(self, t, dram=False):
        self.t = t
        self.w = {}
        self.r = {}
        self.dram = dram
        self.dsem = None
        self.dcnt = 0


class KB:
    def __init__(self, nc, st):
        self.nc, self.st = nc, st
        self.eng = {"pe": nc.tensor, "act": nc.scalar, "dve": nc.vector, "pool": nc.gpsimd, "sp": nc.sync}
        self.sem = {n: st.enter_context(nc.semaphore("s_" + n)) for n in self.eng}
        self.tick = {n: 0 for n in self.eng}
        self.seen = {n: {} for n in self.eng}
        self.semobj = {}
        self.dmax = {}
        self.nbuf = 0

    def sb(self, shape, dt, name=None):
        self.nbuf += 1
        return Buf(self.st.enter_context(self.nc.sbuf_tensor(f"{name or 'b'}{self.nbuf}", list(shape), dt)))

    def ps(self, shape, dt, name=None):
        self.nbuf += 1
        return Buf(self.st.enter_context(self.nc.psum_tensor(f"{name or 'p'}{self.nbuf}", list(shape), dt)))

    def _need(self, e, evs, out):
        for sid, val in evs.items():
            if e == "pe" and sid == "pe":
                continue
            if self.seen[e].get(sid, 0) < val and out.get(sid, 0) < val:
                out[sid] = val

    def _semh(self, sid):
        return self.semobj[sid] if sid in self.semobj else self.sem[sid]

    def _wait(self, e, evs):
        need = {}
        self._need(e, evs, need)
        for sid, val in need.items():
            self.eng[e].wait_ge(self._semh(sid), val)
            self.seen[e][sid] = val

    def _pre(self, e, R, W):
        need = {}
        for b in R:
            self._need(e, b.w, need)
        for b in W:
            self._need(e, b.w, need)
            self._need(e, b.r, need)
        items = list(need.items())
        for sid, val in items[:-1]:
            self.eng[e].wait_ge(self._semh(sid), val)
            self.seen[e][sid] = val
        if items:
            sid, val = items[-1]
            self.seen[e][sid] = val
            return (self._semh(sid), val)
        return None

    def _post(self, sid, val, R, W):
        for b in R:
            b.r[sid] = val
        for b in W:
            if b.dram:
                b.w[sid] = val
            else:
                b.w = {sid: val}
                b.r = {}

    def op(self, e, meth, R, W, *a, **kw):
        fw = self._pre(e, R, W)
        ins = getattr(self.eng[e], meth)(*a, **kw)
        if fw is not None:
            ins._wait_ge(*fw)
        self.tick[e] += 1
        ins.then_inc(self.sem[e], 1)
        self._post(e, self.tick[e], R, W)
        return ins

    def dma(self, q, out_b, out_ap, in_b, in_ap, **kw):
        sbb = in_b if out_b.dram else out_b
        if sbb.dsem is None:
            self.nbuf += 1
            sbb.dsem = f"d{self.nbuf}"
            self.semobj[sbb.dsem] = self.st.enter_context(self.nc.semaphore(sbb.dsem))
        fw = self._pre(q, [in_b], [out_b])
        ins = self.eng[q].dma_start(out=out_ap, in_=in_ap, **kw)
        if fw is not None:
            ins._wait_ge(*fw)
        ins.then_inc(self.semobj[sbb.dsem], 16)
        sbb.dcnt += 16
        self.dmax[sbb.dsem] = sbb.dcnt
        self._post(sbb.dsem, sbb.dcnt, [in_b], [out_b])

    def barrier(self):
        ev = dict(self.tick)
        ev.update(self.dmax)
        for e in self.eng:
            self._wait(e, {a: b for a, b in ev.items() if b > 0})

    def phase(self):
        kb = self

        class _P:
            def __enter__(s2):
                s2.old = kb.st
                s2.es = ExitStack()
                s2.es.__enter__()
                kb.st = s2.es

            def __exit__(s2, *a):
                if a[0] is None:
                    kb.barrier()
                kb.st = s2.old
                return s2.es.__exit__(*a)
        return _P()

    def finish(self, bufs):
        for b in bufs:
            self._wait("sp", b.w)


def build(cfg, dbg=(), seg=None):
    D, S, DEPTH, NMEM = cfg["D"], cfg["S"], cfg["DEPTH"], cfg["NMEM"]
    TBM, TF = cfg["TBM"], cfg["TF"]
    NA, MIXW, H, MEMW, MHD, DFF, ALPHA = (cfg[k] for k in ("NA", "MIXW", "H", "MEMW", "MHD", "DFF", "ALPHA"))
    NCH, NB = D // 128, S // 128
    NFF, MC, MCH, NMB = DFF // 128, MEMW // 128, MHD // 128, NMEM // 128
    AW, BW = 4 * MIXW + MEMW, MIXW + MEMW
    TX = max(TBM, TF, 128)
    nc = bass.Bass("TRN2", target_bir_lowering=False)

    def din(name, shape, dt=F32):
        return Buf(nc.dram_tensor(name, list(shape), dt, kind="ExternalInput"), dram=True)

    def dsc(name, shape, dt=F32, out=False):
        return Buf(nc.dram_tensor(name, list(shape), dt, kind="ExternalOutput" if (out or name in dbg) else "Internal"), dram=True)

    def need(*segs):
        return seg is None or seg in segs

    LA = NA if seg is None else 1
    LB = DEPTH - NA if seg is None else 1
    LD = DEPTH if seg is None else 1
    x_in = din("x", [S, D]) if need("A", "KV", "B") else None
    mem_in = din("mem", [NMEM, D]) if need("A", "B") else None
    a_w_in = din("a_w_in", [LA, D, AW]) if need("A") else None
    lbl = din("hgrn_lb_logits", [NA, MIXW]) if need("A") else None
    ong = din("a_onorm_g", [LA, MIXW]) if need("A") else None
    lbflag = din("lbflag", [128, 1]) if seg == "A" else None
    b_w_in = din("b_w_in", [LB, D, BW]) if need("B") else None
    w_kv = din("w_kv_shared", [D, 2 * MIXW]) if need("KV") else None
    w_mkv = din("w_mem_kv", [LD, D, 2 * MEMW]) if need("A", "B") else None
    w_o = din("w_o", [LD, D, D]) if need("A", "B") else None
    f_w_in = din("ffn_w_in", [LD, D, 2 * DFF]) if need("FFN") else None
    f_cw = din("ffn_conv_w", [LD, 3, DFF]) if need("FFN") else None
    f_cb = din("ffn_conv_b", [LD, DFF]) if need("FFN") else None
    f_w_out = din("ffn_w_out", [LD, DFF, D]) if need("FFN") else None
    ln_g = din("ln_g", [LD, 2, D]) if need("A", "B", "FFN") else None
    ln_b = din("ln_b", [LD, 2, D]) if need("A", "B", "FFN") else None
    c_ident = din("c_ident", [128, 128], BF16); c_mle = din("c_mle", [128, 128], I32)
    c_nU = din("c_nU", [128, 128]); c_nL = din("c_nL", [128, 128])
    c_m01 = din("c_m01", [TBM // 128, 128, TBM]); c_mb = din("c_mb", [TBM // 128, 128, TBM])
    if seg is None:
        y_out = dsc("y", [S, D], out=True)
        xs = [dsc("xs0", [S, D]), dsc("xs1", [S, D])]
        x1 = dsc("x1s", [S, D]); yp = dsc("yps", [S, D])
        KT = dsc("KT", [H, 128, S], BF16); Vd = dsc("Vd", [S, MIXW], BF16)
    else:
        y_out = None; xs = []
        x1 = dsc("x1s", [S, D], out=True) if seg in ("A", "B") else (din("x1s", [S, D]) if seg == "FFN" else None)
        yp = dsc("yps", [S, D]) if seg == "FFN" else None
        xo = dsc("xo", [S, D], out=True) if seg == "FFN" else None
        if seg == "KV":
            KT = dsc("KT", [H, 128, S], BF16, out=True); Vd = dsc("Vd", [S, MIXW], BF16, out=True)
        elif seg == "B":
            KT = din("KT", [H, 128, S], BF16); Vd = din("Vd", [S, MIXW], BF16)
        else:
            KT = Vd = None

    st = ExitStack()
    with st:
        k = KB(nc, st)
        sb, ps = k.sb, k.ps

        def wl(layer):
            return layer if seg is None else 0
        ident = sb([128, 128], BF16); mle = sb([128, 128], I32); nU = sb([128, 128], F32); nL = sb([128, 128], F32)
        m01 = sb([128, TBM // 128, TBM], F32); mb = sb([128, TBM // 128, TBM], F32)
        ones = sb([128, 128], F32); zcol = sb([128, 1], F32)
        k.dma("pool", ident, ident.t[:], c_ident, c_ident.t[:, :])
        k.dma("pool", mle, mle.t[:], c_mle, c_mle.t[:, :])
        k.dma("pool", nU, nU.t[:], c_nU, c_nU.t[:, :])
        k.dma("pool", nL, nL.t[:], c_nL, c_nL.t[:, :])
        k.dma("pool", m01, m01.t[:], c_m01, c_m01.t.ap().rearrange("o p t -> p o t"))
        k.dma("pool", mb, mb.t[:], c_mb, c_mb.t.ap().rearrange("o p t -> p o t"))
        k.op("dve", "memset", [], [ones], ones.t[:], 1.0)
        k.op("dve", "memset", [], [zcol], zcol.t[:], 0.0)
        CG = 256
        WSH = {"ws": None, "cgt": 256}
        wsi = [0]

        def alloc_ws(cgt):
            WSH["ws"] = [sb([128, NCH, cgt], BF16, "ws") for _ in range(2)]
            WSH["cgt"] = cgt
        xT = None
        xld = sb([128, D], F32, "xld"); ysb = sb([128, D], F32, "ysb"); xbf = sb([128, D], BF16, "xbf")
        SEG = 1024 if D % 1024 == 0 else D
        gbc = sb([128, SEG], F32, "gbc"); bbc = sb([128, SEG], F32, "bbc")
        st1 = sb([128, 8], F32, "st1")
        cwT = sb([128, 3, NFF], F32, "cwT"); cbT = sb([128, NFF], F32, "cbT")
        PS = [ps([128, 512], F32, "ps") for _ in range(6)]
        PT = [ps([128, 1024], BF16, "pt") for _ in range(2)]
        psi = [0]; pti = [0]
        pcum, pacc = PS[4], PS[5]

        def nps():
            psi[0] = (psi[0] + 1) % 4; return PS[psi[0]]

        def npt():
            pti[0] = (pti[0] + 1) % len(PT); return PT[pti[0]]

        precast = {}

        def wslab(wb, wap_rows, k0, kn, c0, ncol):
            key = (wb.t.name, wap_rows.offset if hasattr(wap_rows, "offset") else 0, tuple(wap_rows.shape))
            if key not in precast:
                R_, C_ = wap_rows.shape
                k.nbuf += 1
                wbf = Buf(nc.dram_tensor(f"wbf{k.nbuf}", [R_, C_], BF16, kind="Internal"), dram=True)
                for r in range(0, R_, 128):
                    k.dma("pool", wbf, wbf.t[r:r + 128, :], wb, wap_rows[r:r + 128, :], max_dma_last_dim=4096)
                precast[key] = wbf
            wbf = precast[key]
            wsi[0] ^= 1
            s = WSH["ws"][wsi[0]]
            src = wbf.t[k0 * 128:(k0 + kn) * 128, c0:c0 + ncol].rearrange("(c p) m -> p c m", p=128)
            k.dma("sp", s, s.t[:, 0:kn, 0:ncol], wbf, src)
            return s

        def transp(src_b, src_fn, n0, n1, dst_b, dst_fn, eng="dve"):
            for c4 in range(n0, n1, 8):
                pt = npt(); n = min(8, n1 - c4)
                for j in range(n):
                    k.op("pe", "transpose", [src_b, ident], [pt], pt.t[:, j * 128:(j + 1) * 128], src_fn(c4 + j), ident.t[:])
                k.op(eng, "tensor_copy" if eng == "dve" else "copy", [pt], [dst_b], dst_fn(c4, n), pt.t[:, 0:n * 128].rearrange("p (c t) -> p c t", c=n))

        def load_xT(src, r0, T):
            for tb in range(T // 128):
                k.dma("pool", xld, xld.t[:], src, src.t[r0 + tb * 128:r0 + (tb + 1) * 128, :])
                k.op("act", "copy", [xld], [xbf], xbf.t[:], xld.t[:])
                transp(xbf, lambda c: xbf.t[:, c * 128:(c + 1) * 128], 0, NCH, xT, lambda c0, n: xT.t[:, c0:c0 + n, tb * 128:(tb + 1) * 128])

        def gemm_feat(s, j, nk, rhs_b, rhs_fn, N):
            p = nps()
            for kk in range(nk):
                k.op("pe", "matmul", [s, rhs_b], [p], p.t[:, 0:N], s.t[:, kk, j * 128:(j + 1) * 128], rhs_fn(kk), start=(kk == 0), stop=(kk == nk - 1))
            return p

        def tok_gemm(s, nco, tb, out_fn):
            p = nps()
            for kk in range(NCH):
                k.op("pe", "matmul", [xT, s], [p], p.t[:, 0:nco], xT.t[:, kk, tb * 128:(tb + 1) * 128], s.t[:, kk, 0:nco], start=(kk == 0), stop=(kk == NCH - 1))
            out_fn(p)

        def proj_ln(layer, xres, r0, dst, wbuf, wrows, k_lo, k_hi, lhs_b, lhs_fn, gi, mode="full"):
            CGT = WSH["cgt"]
            k.dma("pool", xld, xld.t[:], xres, xres.t[r0:r0 + 128, :])
            for cg in range(D // CGT):
                p = nps()
                for k0 in range(k_lo, k_hi, NCH):
                    kn = min(NCH, k_hi - k0)
                    s = wslab(wbuf, wrows, k0, kn, cg * CGT, CGT)
                    for kk in range(kn):
                        k.op("pe", "matmul", [lhs_b, s], [p], p.t[:, 0:CGT], lhs_fn(k0 + kk), s.t[:, kk, 0:CGT], start=(k0 + kk == k_lo), stop=(k0 + kk == k_hi - 1))
                k.op("dve", "scalar_tensor_tensor", [xld, p], [ysb], out=ysb.t[:, cg * CGT:(cg + 1) * CGT], in0=xld.t[:, cg * CGT:(cg + 1) * CGT],
                     scalar=(1.0 if mode == "last" else float(ALPHA)), in1=p.t[:, 0:CGT], op0=OP.mult, op1=OP.add)
            if mode != "first":
                k.op("dve", "reduce_sum", [ysb], [st1], out=st1.t[:, 0:1], in_=ysb.t[:], axis=AX.X)
                k.op("dve", "tensor_scalar", [st1], [st1], out=st1.t[:, 1:2], in0=st1.t[:, 0:1], scalar1=-1.0 / D, scalar2=None, op0=OP.mult)
                k.op("act", "activation", [ysb, st1], [ysb], out=ysb.t[:], in_=ysb.t[:], func=AF.Identity, bias=st1.t[:, 1:2], scale=1.0)
                k.op("act", "activation", [ysb], [xld, st1], out=xld.t[:], in_=ysb.t[:], func=AF.Square, accum_out=st1.t[:, 2:3])
                k.op("dve", "tensor_scalar", [st1], [st1], out=st1.t[:, 3:4], in0=st1.t[:, 2:3], scalar1=1.0 / D, scalar2=1e-5, op0=OP.mult, op1=OP.add)
                k.op("act", "activation", [st1], [st1], out=st1.t[:, 4:5], in_=st1.t[:, 3:4], func=AF.Ln)
                k.op("act", "activation", [st1], [st1], out=st1.t[:, 4:5], in_=st1.t[:, 4:5], func=AF.Exp, scale=-0.5)
                for sg in range(D // SEG):
                    sl = slice(sg * SEG, (sg + 1) * SEG)
                    k.dma("pool", gbc, gbc.t[:], ln_g, ln_g.t[wl(layer), gi:gi + 1, sl].partition_broadcast(128))
                    k.dma("pool", bbc, bbc.t[:], ln_b, ln_b.t[wl(layer), gi:gi + 1, sl].partition_broadcast(128))
                    k.op("dve", "scalar_tensor_tensor", [ysb, st1, gbc], [ysb], out=ysb.t[:, sl], in0=ysb.t[:, sl], scalar=st1.t[:, 4:5], in1=gbc.t[:], op0=OP.mult, op1=OP.mult)
                    k.op("dve", "tensor_tensor", [ysb, bbc], [ysb], out=ysb.t[:, sl], in0=ysb.t[:, sl], in1=bbc.t[:], op=OP.add)
            k.dma("pool", dst, dst.t[r0:r0 + 128, :], ysb, ysb.t[:])

        def mixer_phase(layer, cur):
            nonlocal xT
            TM = 128 if layer < NA else TBM
            alloc_ws(256)
            xT = sb([128, NCH, TM], BF16, "xT")
            mkT = sb([128, MC, NMEM], BF16, "mkT"); mv = sb([128, NMB, MEMW], BF16, "mv")
            mqT = sb([128, MC, TM], BF16, "mqT"); ocT = sb([128, NCH, TM], BF16, "ocT"); otok = sb([128, D], BF16, "otok")
            pmem = sb([128, NMEM], BF16, "pmem"); pTm = sb([128, NMB, 128], BF16, "pTm")
            with k.phase():
                memT = sb([128, NCH, NMEM], BF16, "memT")
                for mbk in range(NMB):
                    k.dma("pool", xld, xld.t[:], mem_in, mem_in.t[mbk * 128:(mbk + 1) * 128, :])
                    k.op("act", "copy", [xld], [xbf], xbf.t[:], xld.t[:])
                    transp(xbf, lambda c: xbf.t[:, c * 128:(c + 1) * 128], 0, NCH, memT, lambda c0, n: memT.t[:, c0:c0 + n, mbk * 128:(mbk + 1) * 128])
                wr = w_mkv.t[wl(layer)]
                for c0 in range(0, MEMW, CG):
                    nco = min(CG, MEMW - c0)
                    s = wslab(w_mkv, wr, 0, NCH, c0, nco)
                    for j in range(nco // 128):
                        p = gemm_feat(s, j, NCH, memT, lambda kk: memT.t[:, kk, :], NMEM)
                        k.op("act", "copy", [p], [mkT], mkT.t[:, c0 // 128 + j, :], p.t[:, 0:NMEM])
                for c0 in range(0, MEMW, CG):
                    nco = min(CG, MEMW - c0)
                    s = wslab(w_mkv, wr, 0, NCH, MEMW + c0, nco)
                    for mbk in range(NMB):
                        p = nps()
                        for kk in range(NCH):
                            k.op("pe", "matmul", [memT, s], [p], p.t[:, 0:nco], memT.t[:, kk, mbk * 128:(mbk + 1) * 128], s.t[:, kk, 0:nco], start=(kk == 0), stop=(kk == NCH - 1))
                        k.op("act", "copy", [p], [mv], mv.t[:, mbk, c0:c0 + nco], p.t[:, 0:nco])

            def mem_attn(tb):
                sc = float(MHD ** -0.5)
                for h in range(4):
                    p = nps()
                    for c in range(MCH):
                        k.op("pe", "matmul", [mqT, mkT], [p], p.t[:, 0:NMEM], mqT.t[:, h * MCH + c, tb * 128:(tb + 1) * 128], mkT.t[:, h * MCH + c, :], start=(c == 0), stop=(c == MCH - 1))
                    k.op("dve", "reduce_max", [p], [st1], out=st1.t[:, 5:6], in_=p.t[:, 0:NMEM], axis=AX.X)
                    k.op("dve", "tensor_scalar", [st1], [st1], out=st1.t[:, 6:7], in0=st1.t[:, 5:6], scalar1=-sc, scalar2=None, op0=OP.mult)
                    k.op("act", "activation", [p, st1], [pmem, st1], out=pmem.t[:], in_=p.t[:, 0:NMEM], func=AF.Exp, bias=st1.t[:, 6:7], scale=sc, accum_out=st1.t[:, 7:8])
                    transp(pmem, lambda m: pmem.t[:, m * 128:(m + 1) * 128], 0, NMB, pTm, lambda c0, n: pTm.t[:, c0:c0 + n, :], eng="act")
                    p2 = nps()
                    for m in range(NMB):
                        k.op("pe", "matmul", [pTm, mv], [p2], p2.t[:, 0:MHD], pTm.t[:, m, :], mv.t[:, m, h * MHD:(h + 1) * MHD], start=(m == 0), stop=(m == NMB - 1))
                    k.op("dve", "reciprocal", [st1], [st1], out=st1.t[:, 5:6], in_=st1.t[:, 7:8])
                    k.op("dve", "tensor_scalar", [p2, st1], [otok], out=otok.t[:, MIXW + h * MHD:MIXW + (h + 1) * MHD], in0=p2.t[:, 0:MHD], scalar1=st1.t[:, 5:6], scalar2=None, op0=OP.mult)

            def finish_block(r0, tb, c_lo, c_hi):
                mem_attn(tb)
                transp(otok, lambda c: otok.t[:, c * 128:(c + 1) * 128], c_lo, c_hi, ocT, lambda c0, n: ocT.t[:, c0:c0 + n, tb * 128:(tb + 1) * 128], eng="act")
                proj_ln(layer, cur, r0 + tb * 128, x1, w_o, w_o.t[wl(layer)], 0, NCH, ocT, lambda kk: ocT.t[:, kk, tb * 128:(tb + 1) * 128], 0)

            if layer < NA:
                a_mixer(layer, cur, mqT, otok, finish_block)
            else:
                b_mixer(layer, cur, mqT, ocT, finish_block)

        def a_mixer(layer, cur, mqT, otok, finish_block):
            TT = 128
            Sst = sb([128, H, 128], F32, "Sst"); Sbf = sb([128, H, 128], BF16, "Sbf")
            lbT = sb([128, H], F32, "lbT"); omlT = sb([128, H], F32, "omlT"); lb2 = sb([128, 2, H], F32, "lb2")
            ongb = sb([128, CG], F32, "ongb")
            vtok = sb([128, MIXW], BF16, "vtok"); gg = sb([128, MIXW], F32, "gg")
            qs = sb([128, TT], F32, "qs"); fk = sb([128, TT], F32, "fk"); kk_ = sb([128, TT], F32, "kk"); Bc = sb([128, TT], F32, "Bc"); nB = sb([128, TT], F32, "nB")
            E = sb([128, 128], F32, "E"); qt_ = sb([128, 128], BF16, "qt"); kt_ = sb([128, 128], BF16, "kt"); qh = sb([128, 128], BF16, "qh"); kh = sb([128, 128], BF16, "kh")
            khT = sb([128, 128], BF16, "khT"); Pm = sb([128, 128], BF16, "Pm"); dec = sb([128, 4], F32, "dec"); osq = sb([128, 128], F32, "osq")
            for l_ in range(2):
                k.dma("pool", lb2, lb2.t[:, l_, :], lbl, lbl.t[l_, :].rearrange("(h p) -> p h", p=128), allow_slow_non_contiguous=True)
            if seg is None and layer == 0:
                k.op("dve", "memset", [], [lbT], lbT.t[:], 0.0)
            else:
                k.op("dve", "tensor_tensor", [lb2], [lbT], out=lbT.t[:], in0=lb2.t[:, 1, :], in1=lb2.t[:, 0, :], op=OP.subtract)
                k.op("act", "activation", [lbT], [lbT], out=lbT.t[:], in_=lbT.t[:], func=AF.Sigmoid)
                if seg is not None:
                    flg = sb([128, 1], F32, "flg")
                    k.dma("pool", flg, flg.t[:], lbflag, lbflag.t[:, :])
                    k.op("dve", "tensor_scalar", [lbT, flg], [lbT], out=lbT.t[:], in0=lbT.t[:], scalar1=flg.t[:, 0:1], scalar2=None, op0=OP.mult)
            k.op("dve", "tensor_scalar", [lbT], [omlT], out=omlT.t[:], in0=lbT.t[:], scalar1=-1.0, scalar2=1.0, op0=OP.mult, op1=OP.add)
            k.op("dve", "memset", [], [Sst], Sst.t[:], 0.0)
            k.op("dve", "memset", [], [Sbf], Sbf.t[:], 0.0)
            wr = a_w_in.t[wl(layer)]
            xTa = lambda kk: xT.t[:, kk, 0:TT]
            for ti in range(S // TT):
                r0 = ti * TT
                load_xT(cur, r0, TT)
                for c0 in range(0, MIXW, CG):
                    nco = min(CG, MIXW - c0)
                    s = wslab(a_w_in, wr, 0, NCH, 2 * MIXW + c0, nco)
                    tok_gemm(s, nco, 0, lambda p: k.op("act", "copy", [p], [vtok], vtok.t[:, c0:c0 + nco], p.t[:, 0:nco]))
                    s = wslab(a_w_in, wr, 0, NCH, 3 * MIXW + c0, nco)
                    k.dma("pool", ongb, ongb.t[:, 0:nco], ong, ong.t[wl(layer):wl(layer) + 1, c0:c0 + nco].partition_broadcast(128))

                    def gout(p):
                        k.op("act", "activation", [p], [gg], out=gg.t[:, c0:c0 + nco], in_=p.t[:, 0:nco], func=AF.Silu)
                        k.op("dve", "tensor_tensor", [gg, ongb], [gg], out=gg.t[:, c0:c0 + nco], in0=gg.t[:, c0:c0 + nco], in1=ongb.t[:, 0:nco], op=OP.mult)
                    tok_gemm(s, nco, 0, gout)
                for c0 in range(0, MEMW, CG):
                    nco = min(CG, MEMW - c0)
                    s = wslab(a_w_in, wr, 0, NCH, 4 * MIXW + c0, nco)
                    for j in range(nco // 128):
                        p = gemm_feat(s, j, NCH, xT, xTa, TT)
                        k.op("act", "copy", [p], [mqT], mqT.t[:, c0 // 128 + j, :], p.t[:, 0:TT])
                HG = CG // 128
                for h4 in range(0, H, HG):
                    nh = min(HG, H - h4)
                    sq = wslab(a_w_in, wr, 0, NCH, h4 * 128, nh * 128)
                    sf = wslab(a_w_in, wr, 0, NCH, MIXW + h4 * 128, nh * 128)
                    for j in range(nh):
                        h = h4 + j
                        pq = gemm_feat(sq, j, NCH, xT, xTa, TT)
                        k.op("act", "activation", [pq], [qs], out=qs.t[:], in_=pq.t[:, 0:TT], func=AF.Silu)
                        pf = gemm_feat(sf, j, NCH, xT, xTa, TT)
                        k.op("act", "activation", [pf], [fk], out=fk.t[:], in_=pf.t[:, 0:TT], func=AF.Sigmoid)
                        k.op("dve", "tensor_scalar", [fk, omlT, lbT], [fk], out=fk.t[:], in0=fk.t[:], scalar1=omlT.t[:, h:h + 1], scalar2=lbT.t[:, h:h + 1], op0=OP.mult, op1=OP.add)
                        k.op("dve", "tensor_scalar", [fk], [kk_], out=kk_.t[:], in0=fk.t[:], scalar1=-1.0, scalar2=1.0, op0=OP.mult, op1=OP.add)
                        k.op("act", "activation", [fk], [fk], out=fk.t[:], in_=fk.t[:], func=AF.Ln)
                        k.op("dve", "tensor_tensor_scan", [ones, fk], [Bc], out=Bc.t[:], data0=ones.t[:], data1=fk.t[:], initial=0.0, op0=OP.mult, op1=OP.add)
                        k.op("dve", "tensor_scalar", [Bc], [nB], out=nB.t[:], in0=Bc.t[:], scalar1=-1.0, scalar2=None, op0=OP.mult)
                        rb = [Bc, nB]
                        k.op("act", "activation", rb, [E], out=E.t[:], in_=Bc.t[:], func=AF.Exp, bias=nB.t[:, 63:64], scale=1.0)
                        k.op("dve", "tensor_tensor", [qs, E], [qt_], out=qt_.t[:], in0=qs.t[:], in1=E.t[:], op=OP.mult)
                        k.op("act", "activation", rb, [E], out=E.t[:], in_=Bc.t[:], func=AF.Exp, bias=Bc.t[:, 63:64], scale=-1.0)
                        k.op("dve", "tensor_tensor", [kk_, E], [kt_], out=kt_.t[:], in0=kk_.t[:], in1=E.t[:], op=OP.mult)
                        k.op("act", "activation", rb, [E], out=E.t[:], in_=Bc.t[:], func=AF.Exp)
                        k.op("dve", "tensor_tensor", [qs, E], [qh], out=qh.t[:], in0=qs.t[:], in1=E.t[:], op=OP.mult)
                        k.op("act", "activation", rb, [E], out=E.t[:], in_=Bc.t[:], func=AF.Exp, bias=Bc.t[:, 127:128], scale=-1.0)
                        k.op("dve", "tensor_tensor", [kk_, E], [kh], out=kh.t[:], in0=kk_.t[:], in1=E.t[:], op=OP.mult)
                        k.op("act", "activation", rb, [dec], out=dec.t[:, 0:1], in_=Bc.t[:, 127:128], func=AF.Exp)
                        p = nps()
                        k.op("pe", "matmul", [kt_, qt_], [p], p.t[:, 0:128], kt_.t[:], qt_.t[:], start=True, stop=True)
                        k.op("dve", "memset", [], [Pm], Pm.t[:], 0.0)
                        k.op("dve", "copy_predicated", [p, mle], [Pm], out=Pm.t[:], mask=mle.t[:], data=p.t[:, 0:128])
                        pt = npt()
                        k.op("pe", "transpose", [kh, ident], [pt], pt.t[:, 0:128], kh.t[:], ident.t[:])
                        k.op("act", "copy", [pt], [khT], khT.t[:], pt.t[:, 0:128])
                        po = nps()
                        k.op("pe", "matmul", [Pm, vtok], [po], po.t[:, 0:128], Pm.t[:], vtok.t[:, h * 128:(h + 1) * 128], start=True, stop=False)
                        k.op("pe", "matmul", [qh, Sbf], [po], po.t[:, 0:128], qh.t[:], Sbf.t[:, h, :], start=False, stop=True)
                        pd = nps()
                        k.op("pe", "matmul", [khT, vtok], [pd], pd.t[:, 0:128], khT.t[:], vtok.t[:, h * 128:(h + 1) * 128], start=True, stop=True)
                        k.op("dve", "scalar_tensor_tensor", [Sst, dec, pd], [Sst], out=Sst.t[:, h, :], in0=Sst.t[:, h, :], scalar=dec.t[:, 0:1], in1=pd.t[:, 0:128], op0=OP.mult, op1=OP.add)
                        k.op("act", "copy", [Sst], [Sbf], Sbf.t[:, h, :], Sst.t[:, h, :])
                        k.op("act", "activation", [po], [osq, dec], out=osq.t[:], in_=po.t[:, 0:128], func=AF.Square, accum_out=dec.t[:, 1:2])
                        k.op("dve", "tensor_scalar", [dec], [dec], out=dec.t[:, 2:3], in0=dec.t[:, 1:2], scalar1=1.0 / 128, scalar2=1e-6, op0=OP.mult, op1=OP.add)
                        k.op("act", "activation", [dec], [dec], out=dec.t[:, 3:4], in_=dec.t[:, 2:3], func=AF.Ln)
                        k.op("act", "activation", [dec], [dec], out=dec.t[:, 3:4], in_=dec.t[:, 3:4], func=AF.Exp, scale=-0.5)
                        k.op("dve", "scalar_tensor_tensor", [po, dec, gg], [otok], out=otok.t[:, h * 128:(h + 1) * 128], in0=po.t[:, 0:128], scalar=dec.t[:, 3:4],
                             in1=gg.t[:, h * 128:(h + 1) * 128], op0=OP.mult, op1=OP.mult)
                finish_block(r0, 0, 0, NCH)

        def kv_body(cur, wsb):
            TT = TBM; TBB = TT // 128
            xTa = lambda kk: xT.t[:, kk, 0:TT]
            with k.phase():
                    vtok = sb([128, MIXW], BF16, "vtokb")
                    for ti in range(S // TT):
                        r0 = ti * TT
                        load_xT(cur, r0, TT)
                        for c0 in range(0, MIXW, CG):
                            nco = min(CG, MIXW - c0)
                            s = wslab(w_kv, w_kv.t, 0, NCH, c0, nco)
                            for j in range(nco // 128):
                                h = c0 // 128 + j
                                p = gemm_feat(s, j, NCH, xT, xTa, TT)
                                k.op("act", "copy", [p], [wsb], wsb.t[:], p.t[:, 0:TT])
                                k.dma("pool", KT, KT.t[h, :, r0:r0 + TT], wsb, wsb.t[:])
                        for tb in range(TBB):
                            for c0 in range(0, MIXW, CG):
                                nco = min(CG, MIXW - c0)
                                s = wslab(w_kv, w_kv.t, 0, NCH, MIXW + c0, nco)
                                tok_gemm(s, nco, tb, lambda p: k.op("act", "copy", [p], [vtok], vtok.t[:, c0:c0 + nco], p.t[:, 0:nco]))
                            k.dma("pool", Vd, Vd.t[r0 + tb * 128:r0 + (tb + 1) * 128, :], vtok, vtok.t[:])

        def kv_phase(cur):
            nonlocal xT
            alloc_ws(256)
            xT = sb([128, NCH, TBM], BF16, "xT")
            wsb = sb([128, TBM], BF16, "wsbk")
            kv_body(cur, wsb)

        def b_mixer(layer, cur, mqT, ocT, finish_block):
            TT = TBM; TBB = TT // 128; GK = 8
            li = layer - NA
            qTb = sb([128, H, TT], BF16, "qTb")
            KTh = sb([128, GK * 128], BF16, "KTh"); Vh = sb([128, GK, 128], BF16, "Vh")
            esb = sb([128, TT], F32, "esb"); spb = sb([128, TT], F32, "spb"); t1 = sb([128, TT], F32, "t1"); wsb = sb([128, TT], BF16, "wsb")
            xTa = lambda kk: xT.t[:, kk, 0:TT]
            if seg is None and layer == NA:
                kv_body(cur, wsb)
            wr = b_w_in.t[li if seg is None else 0]
            sc = float(128 ** -0.5)
            for ti in range(S // TT):
                r0 = ti * TT
                load_xT(cur, r0, TT)
                for c0 in range(0, BW, CG):
                    nco = min(CG, BW - c0)
                    s = wslab(b_w_in, wr, 0, NCH, c0, nco)
                    for j in range(nco // 128):
                        cc = c0 // 128 + j
                        p = gemm_feat(s, j, NCH, xT, xTa, TT)
                        if cc < H:
                            k.op("act", "copy", [p], [qTb], qTb.t[:, cc, :], p.t[:, 0:TT])
                        else:
                            k.op("act", "copy", [p], [mqT], mqT.t[:, cc - H, :], p.t[:, 0:TT])
                kb0 = ti * TBB
                kbmax = kb0 + TBB - 1
                for h in range(H):
                    for kb in range(kbmax, -1, -1):
                        g0 = (kb // GK) * GK
                        if kb == kbmax or kb % GK == GK - 1:
                            ng = kb - g0 + 1
                            k.dma("pool", KTh, KTh.t[:, 0:ng * 128], KT, KT.t[h, :, g0 * 128:(g0 + ng) * 128])
                            k.dma("pool", Vh, Vh.t[:, 0:ng, :], Vd, Vd.t[g0 * 128:(g0 + ng) * 128, h * 128:(h + 1) * 128].rearrange("(b p) d -> p b d", p=128))
                        kl = kb - g0
                        o = kb - kb0
                        pz = nps()
                        k.op("pe", "matmul", [KTh, qTb], [pz], pz.t[:, 0:TT], KTh.t[:, kl * 128:(kl + 1) * 128], qTb.t[:, h, :], start=True, stop=True)
                        k.op("act", "activation", [pz], [esb], out=esb.t[:], in_=pz.t[:, 0:TT], func=AF.Exp, scale=sc)
                        k.op("act", "activation", [esb], [spb], out=spb.t[:], in_=esb.t[:], func=AF.Ln, bias=1.0, scale=1.0)
                        k.op("dve", "scalar_tensor_tensor", [pz, spb], [t1], out=t1.t[:], in0=pz.t[:, 0:TT], scalar=sc, in1=spb.t[:], op0=OP.mult, op1=OP.subtract)
                        if o >= 0:
                            k.op("dve", "tensor_tensor", [spb, m01], [spb], out=spb.t[:], in0=spb.t[:], in1=m01.t[:, o, :], op=OP.mult)
                        k.op("pe", "matmul", [nU, spb], [pcum], pcum.t[:, 0:TT], nU.t[:], spb.t[:], start=(kb == kbmax), stop=False, skip_group_check=True)
                        k.op("dve", "tensor_tensor", [t1, pcum], [t1], out=t1.t[:], in0=t1.t[:], in1=pcum.t[:, 0:TT], op=OP.add)
                        if o >= 0:
                            k.op("dve", "tensor_tensor", [t1, mb], [t1], out=t1.t[:], in0=t1.t[:], in1=mb.t[:, o, :], op=OP.add)
                        k.op("pe", "matmul", [nL, spb], [pcum], pcum.t[:, 0:TT], nL.t[:], spb.t[:], start=False, stop=(kb == 0), skip_group_check=True)
                        k.op("act", "activation", [t1], [wsb], out=wsb.t[:], in_=t1.t[:], func=AF.Exp)
                        k.op("pe", "matmul", [Vh, wsb], [pacc], pacc.t[:, 0:TT], Vh.t[:, kl, :], wsb.t[:], start=(kb == kbmax), stop=(kb == 0), skip_group_check=True)
                    k.op("act", "copy", [pacc], [ocT], ocT.t[:, h, :], pacc.t[:, 0:TT])
                for tb in range(TBB):
                    finish_block(r0, tb, H, H + MC)

        def ffn_phase(layer, dst):
            nonlocal xT
            TT = TF
            alloc_ws(512)
            xT = sb([128, NCH, TT], BF16, "xT")
            NH = (NFF + 1) // 2 if NFF > 8 else NFF
            halves = [(0, NH)] + ([(NH, NFF)] if NH < NFF else [])
            hT = sb([128, NH, TT], BF16, "hT")
            gcar = sb([128, NFF, 2], F32, "gcar")
            gsb = sb([128, TT + 2], F32, "gsb"); csb = sb([128, TT], F32, "csb")
            xTa = lambda kk: xT.t[:, kk, 0:TT]
            for f0 in range(0, NFF, 16):
                f1 = min(NFF, f0 + 16)
                for tap in range(3):
                    k.dma("pool", cwT, cwT.t[:, tap, f0:f1], f_cw, f_cw.t[wl(layer), tap, f0 * 128:f1 * 128].rearrange("(f p) -> p f", p=128), allow_slow_non_contiguous=True)
                k.dma("pool", cbT, cbT.t[:, f0:f1], f_cb, f_cb.t[wl(layer), f0 * 128:f1 * 128].rearrange("(f p) -> p f", p=128), allow_slow_non_contiguous=True)
            k.op("dve", "memset", [], [gcar], gcar.t[:], 0.0)
            wr = f_w_in.t[wl(layer)]
            for ti in range(S // TT):
                r0 = ti * TT
                load_xT(x1, r0, TT)
                for hi, (fa, fb) in enumerate(halves):
                    for c0 in range(fa * 128, fb * 128, CG):
                        nco = min(CG, fb * 128 - c0)
                        sg = wslab(f_w_in, wr, 0, NCH, c0, nco)
                        su = wslab(f_w_in, wr, 0, NCH, DFF + c0, nco)
                        for j in range(nco // 128):
                            f = c0 // 128 + j
                            pg = gemm_feat(sg, j, NCH, xT, xTa, TT)
                            k.op("act", "copy", [gcar], [gsb], gsb.t[:, 0:2], gcar.t[:, f, :])
                            k.op("act", "copy", [pg], [gsb], gsb.t[:, 2:2 + TT], pg.t[:, 0:TT])
                            k.op("act", "copy", [gsb], [gcar], gcar.t[:, f, :], gsb.t[:, TT:TT + 2])
                            k.op("dve", "tensor_scalar", [gsb, cwT, cbT], [csb], out=csb.t[:], in0=gsb.t[:, 0:TT], scalar1=cwT.t[:, 0, f:f + 1], scalar2=cbT.t[:, f:f + 1], op0=OP.mult, op1=OP.add)
                            k.op("dve", "scalar_tensor_tensor", [gsb, cwT, csb], [csb], out=csb.t[:], in0=gsb.t[:, 1:1 + TT], scalar=cwT.t[:, 1, f:f + 1], in1=csb.t[:], op0=OP.mult, op1=OP.add)
                            k.op("dve", "scalar_tensor_tensor", [gsb, cwT, csb], [csb], out=csb.t[:], in0=gsb.t[:, 2:2 + TT], scalar=cwT.t[:, 2, f:f + 1], in1=csb.t[:], op0=OP.mult, op1=OP.add)
                            k.op("act", "activation", [csb], [csb], out=csb.t[:], in_=csb.t[:], func=AF.Silu)
                            pu = gemm_feat(su, j, NCH, xT, xTa, TT)
                            k.op("dve", "tensor_tensor", [csb, pu], [hT], out=hT.t[:, f - fa, :], in0=csb.t[:], in1=pu.t[:, 0:TT], op=OP.mult)
                    mode = "full" if len(halves) == 1 else ("first" if hi == 0 else "last")
                    for tb in range(TT // 128):
                        proj_ln(layer, yp if mode == "last" else x1, r0 + tb * 128, yp if mode == "first" else dst, f_w_out, f_w_out.t[wl(layer)], fa, fb,
                                hT, lambda kk: hT.t[:, kk - fa, tb * 128:(tb + 1) * 128], 1, mode=mode)

        if seg is None:
            cur = x_in
            for layer in range(DEPTH):
                dst = y_out if layer == DEPTH - 1 else xs[layer % 2]
                with k.phase():
                    mixer_phase(layer, cur)
                with k.phase():
                    ffn_phase(layer, dst)
                cur = dst
            k.finish([y_out] + [b for b in (xs + [x1, KT, Vd]) if b.t.name in dbg])
        elif seg == "A":
            with k.phase():
                mixer_phase(0, x_in)
            k.finish([x1])
        elif seg == "KV":
            with k.phase():
                kv_phase(x_in)
            k.finish([KT, Vd])
        elif seg == "B":
            with k.phase():
                mixer_phase(NA + 1, x_in)
            k.finish([x1])
        elif seg == "FFN":
            with k.phase():
                ffn_phase(0, xo)
            k.finish([xo])
    return nc


def _consts(cfg):
    TT = cfg["TBM"]; TB = TT // 128
    s = np.arange(128)[:, None]; t = np.arange(128)[None, :]
    c = {}
    c["c_ident"] = np.eye(128, dtype=np.float32).astype(ml_dtypes.bfloat16)
    c["c_mle"] = (s <= t).astype(np.int32)
    U = (s > t).astype(np.float32)
    c["c_nU"] = -U
    c["c_nL"] = -(1.0 - U)
    tt = np.arange(TT)[None, :]
    m = np.stack([((s + 128 * o) < tt) for o in range(TB)]).astype(np.float32)
    c["c_m01"] = m
    c["c_mb"] = (m - 1.0) * 30000.0
    return c


def run(cfg, inputs, dbg=()):
    nc = build(cfg, dbg)
    im = {}
    for kname, v in inputs.items():
        a = np.ascontiguousarray(np.asarray(v, dtype=np.float32))
        if kname in ("x", "mem"):
            a = a.reshape(a.shape[-2], a.shape[-1])
        im[kname] = a
    im.update(_consts(cfg))
    res = run_bass_kernel_spmd(nc, [im], core_ids=[0])
    return res.results[0]


def run_segments(cfg, inputs):
    f32 = lambda v: np.ascontiguousarray(np.asarray(v, dtype=np.float32))
    W = {kname: f32(v) for kname, v in inputs.items()}
    S, D, NA, DEPTH = cfg["S"], cfg["D"], cfg["NA"], cfg["DEPTH"]
    consts = _consts(cfg)
    progs = {}

    def launch(seg, im):
        if seg not in progs:
            progs[seg] = build(cfg, seg=seg)
        im = dict(im)
        im.update(consts)
        return run_bass_kernel_spmd(progs[seg], [im], core_ids=[0]).results[0]

    x = W["x"].reshape(S, D)
    mem = W["mem"].reshape(cfg["NMEM"], D)
    KTa = Vda = None
    for layer in range(DEPTH):
        lw = dict(w_mem_kv=W["w_mem_kv"][layer:layer + 1], w_o=W["w_o"][layer:layer + 1],
                  ln_g=W["ln_g"][layer:layer + 1], ln_b=W["ln_b"][layer:layer + 1])
        if layer < NA:
            flag = np.full((128, 1), 1.0 if layer > 0 else 0.0, np.float32)
            r = launch("A", dict(x=x, mem=mem, a_w_in=W["a_w_in"][layer:layer + 1], hgrn_lb_logits=W["hgrn_lb_logits"],
                                 a_onorm_g=W["a_onorm_g"][layer:layer + 1], lbflag=flag, **lw))
        else:
            if layer == NA:
                r = launch("KV", dict(x=x, w_kv_shared=W["w_kv_shared"]))
                KTa, Vda = r["KT"], r["Vd"]
            r = launch("B", dict(x=x, mem=mem, b_w_in=W["b_w_in"][layer - NA:layer - NA + 1], KT=KTa, Vd=Vda, **lw))
        x1 = r["x1s"]
        r = launch("FFN", dict(x1s=x1, ffn_w_in=W["ffn_w_in"][layer:layer + 1], ffn_conv_w=W["ffn_conv_w"][layer:layer + 1],
                               ffn_conv_b=W["ffn_conv_b"][layer:layer + 1], ffn_w_out=W["ffn_w_out"][layer:layer + 1],
                               ln_g=W["ln_g"][layer:layer + 1], ln_b=W["ln_b"][layer:layer + 1]))
        x = r["xo"]
    return x


def kernel(**inputs):
    cfg = make_cfg()
    out = run_segments(cfg, inputs)
    return np.asarray(out, dtype=np.float32).reshape(1, cfg["S"], cfg["D"])
```
